# Optimizing a Trainium2 kernel written in Bass

```python
import jax, jax.numpy as jnp
from jax import lax
import numpy as np

D_MODEL = 1024
BATCH = 8
SEQ = 2048
DEPTH = 2
DEC_BATCH = 128
DEC_SEQ = 1
PAST_LEN = 16384
PAGE_SIZE = 128

CHUNK = 128
A_GROUPS = 4
A_WIDTH = D_MODEL // 4
A_KERNEL = 3
B_HEADS = 4
B_HEAD_DIM = D_MODEL // 8
B_WIDTH = B_HEADS * B_HEAD_DIM
C_GROUPS = 4
C_WIDTH = D_MODEL // 4
C_KERNEL = 31
D_MIX = A_WIDTH + B_WIDTH + C_WIDTH
D_IN = 3 * A_WIDTH + 2 * B_WIDTH + 2 * C_WIDTH
SPLITS = [A_WIDTH, 2 * A_WIDTH, 3 * A_WIDTH,
          3 * A_WIDTH + B_WIDTH, 3 * A_WIDTH + 2 * B_WIDTH,
          3 * A_WIDTH + 2 * B_WIDTH + C_WIDTH]
D_FF = 256 * ((8 * D_MODEL // 3 + 255) // 256)
PLE_DIM = 256
EPS = 1e-6

kernel_name = "hybrid_conv_gmlp_conformer_decode_step"


def rms_norm(x, g):
    xf = x.astype(jnp.float32)
    y = xf * lax.rsqrt(jnp.mean(xf * xf, axis=-1, keepdims=True) + EPS)
    return (y * g.astype(jnp.float32)).astype(x.dtype)


def layer_norm(x, g, b):
    xf = x.astype(jnp.float32)
    mu = jnp.mean(xf, axis=-1, keepdims=True)
    xc = xf - mu
    var = jnp.mean(xc * xc, axis=-1, keepdims=True)
    y = xc * lax.rsqrt(var + EPS) * g.astype(jnp.float32) + b.astype(jnp.float32)
    return y.astype(x.dtype)


def swiglu(h, wg, wu, wd):
    return (jax.nn.silu(h @ wg) * (h @ wu)) @ wd


def depthwise_causal_conv(full, w):
    c = w.shape[1]
    return lax.conv_general_dilated(full, w[:, None, :], window_strides=(1,), padding='VALID',
                                    dimension_numbers=('NWC', 'WIO', 'NWC'),
                                    feature_group_count=c)


def chunk_spatial_mix(v, ws, bias):
    n, t, h, d = v.shape
    L = min(t, CHUNK)
    mask = jnp.tril(jnp.ones((L, L), dtype=bool))
    w = jnp.where(mask[None], ws[:, :L, :L], 0)
    vc = v.reshape(n, t // L, L, h, d)
    out = jnp.einsum('hij,ncjhd->ncihd', w, vc)
    out = out + jnp.swapaxes(bias[:, :L], 0, 1)[None, None, :, :, None]
    return out.reshape(n, t, h, d)


def trunk_layer(x, p, hist_a, hist_c, lw):
    (f1_pre, f1_post, f1_wg, f1_wu, f1_wd, m_pre, m_post, w_in, w_out, a_cw,
     b_lng, b_lnb, b_ws, b_bias, c_cw, c_cb, c_lng, c_lnb,
     f2_pre, f2_post, f2_wg, f2_wu, f2_wd, e_pre, e_post, e_wg, e_wp) = lw
    n, t, _ = x.shape
    x = x + 0.5 * rms_norm(swiglu(rms_norm(x, f1_pre), f1_wg, f1_wu, f1_wd), f1_post)
    h = rms_norm(x, m_pre)
    z = h @ w_in
    a_val, a_c, a_b, b_u, b_v, c_val, c_gate = jnp.split(z, SPLITS, axis=-1)
    fa = jnp.concatenate([hist_a, a_c * a_val], axis=1)
    y_a = a_b * depthwise_causal_conv(fa, a_cw)
    new_a = fa[:, -(A_KERNEL - 1):]
    u = jax.nn.gelu(b_u)
    v = layer_norm(jax.nn.gelu(b_v).reshape(n, t, B_HEADS, B_HEAD_DIM), b_lng, b_lnb)
    y_b = u * chunk_spatial_mix(v, b_ws, b_bias).reshape(n, t, B_WIDTH)
    fc = jnp.concatenate([hist_c, c_val * jax.nn.sigmoid(c_gate)], axis=1)
    y_c = jax.nn.silu(layer_norm(depthwise_causal_conv(fc, c_cw) + c_cb, c_lng, c_lnb))
    new_c = fc[:, -(C_KERNEL - 1):]
    mix = jnp.concatenate([y_a, y_b, y_c], axis=-1) @ w_out
    x = x + rms_norm(mix, m_post)
    x = x + 0.5 * rms_norm(swiglu(rms_norm(x, f2_pre), f2_wg, f2_wu, f2_wd), f2_post)
    gate = jax.nn.sigmoid(rms_norm(x, e_pre) @ e_wg)
    x = x + rms_norm(gate * (p @ e_wp), e_post)
    return x, new_a, new_c, v.reshape(n, t, B_WIDTH)


def setup_inputs(seed: int = 0) -> dict:
    key = jax.random.key(seed)
    keys = iter(jax.random.split(key, 48))

    def nrm(shape, scale):
        return jax.random.normal(next(keys), shape, jnp.float32) * scale

    def gain(shape):
        return 1.0 + nrm(shape, 0.05)

    L = DEPTH
    return {
        "x_prompt": nrm((BATCH, SEQ, D_MODEL), 1.0),
        "x_sample": nrm((DEC_BATCH, DEC_SEQ, D_MODEL), 1.0),
        "state_conv_a": nrm((L, DEC_BATCH, A_KERNEL - 1, A_WIDTH), 1.0),
        "state_conv_c": nrm((L, DEC_BATCH, C_KERNEL - 1, C_WIDTH), 1.0),
        "p_prompt": nrm((L, BATCH, SEQ, PLE_DIM), 1.0),
        "p_sample": nrm((L, DEC_BATCH, DEC_SEQ, PLE_DIM), 1.0),
        "f1_pre": gain((L, D_MODEL)),
        "f1_post": gain((L, D_MODEL)),
        "f1_wg": nrm((L, D_MODEL, D_FF), D_MODEL ** -0.5),
        "f1_wu": nrm((L, D_MODEL, D_FF), D_MODEL ** -0.5),
        "f1_wd": nrm((L, D_FF, D_MODEL), D_FF ** -0.5),
        "m_pre": gain((L, D_MODEL)),
        "m_post": gain((L, D_MODEL)),
        "w_in": nrm((L, D_MODEL, D_IN), D_MODEL ** -0.5),
        "w_out": nrm((L, D_MIX, D_MODEL), D_MIX ** -0.5),
        "a_conv_w": nrm((L, A_KERNEL, A_WIDTH), A_KERNEL ** -0.5),
        "b_ln_g": gain((L, B_HEADS, B_HEAD_DIM)),
        "b_ln_b": nrm((L, B_HEADS, B_HEAD_DIM), 0.02),
        "b_ws": nrm((L, B_HEADS, CHUNK, CHUNK), CHUNK ** -0.5),
        "b_bias": 1.0 + nrm((L, B_HEADS, CHUNK), 0.02),
        "c_conv_w": nrm((L, C_KERNEL, C_WIDTH), C_KERNEL ** -0.5),
        "c_conv_b": nrm((L, C_WIDTH), 0.02),
        "c_ln_g": gain((L, C_WIDTH)),
        "c_ln_b": nrm((L, C_WIDTH), 0.02),
        "f2_pre": gain((L, D_MODEL)),
        "f2_post": gain((L, D_MODEL)),
        "f2_wg": nrm((L, D_MODEL, D_FF), D_MODEL ** -0.5),
        "f2_wu": nrm((L, D_MODEL, D_FF), D_MODEL ** -0.5),
        "f2_wd": nrm((L, D_FF, D_MODEL), D_FF ** -0.5),
        "e_pre": gain((L, D_MODEL)),
        "e_post": gain((L, D_MODEL)),
        "e_wg": nrm((L, D_MODEL, D_MODEL), D_MODEL ** -0.5),
        "e_wp": nrm((L, PLE_DIM, D_MODEL), PLE_DIM ** -0.5),
    }


def reference(x_prompt, x_sample, state_conv_a, state_conv_c, p_prompt, p_sample,
              f1_pre, f1_post, f1_wg, f1_wu, f1_wd, m_pre, m_post, w_in, w_out, a_conv_w,
              b_ln_g, b_ln_b, b_ws, b_bias, c_conv_w, c_conv_b, c_ln_g, c_ln_b,
              f2_pre, f2_post, f2_wg, f2_wu, f2_wd, e_pre, e_post, e_wg, e_wp):
    zeros_a = jnp.zeros((x_prompt.shape[0], A_KERNEL - 1, A_WIDTH), x_prompt.dtype)
    zeros_c = jnp.zeros((x_prompt.shape[0], C_KERNEL - 1, C_WIDTH), x_prompt.dtype)
    yp, ys = x_prompt, x_sample
    a_p, c_p, a_s, c_s, v_s = [], [], [], [], []
    for i in range(DEPTH):
        lw = tuple(w[i] for w in (f1_pre, f1_post, f1_wg, f1_wu, f1_wd, m_pre, m_post, w_in,
                                  w_out, a_conv_w, b_ln_g, b_ln_b, b_ws, b_bias, c_conv_w,
                                  c_conv_b, c_ln_g, c_ln_b, f2_pre, f2_post, f2_wg, f2_wu,
                                  f2_wd, e_pre, e_post, e_wg, e_wp))
        yp, na_p, nc_p, _ = trunk_layer(yp, p_prompt[i], zeros_a, zeros_c, lw)
        ys, na_s, nc_s, vr_s = trunk_layer(ys, p_sample[i], state_conv_a[i], state_conv_c[i], lw)
        a_p.append(na_p)
        c_p.append(nc_p)
        a_s.append(na_s)
        c_s.append(nc_s)
        v_s.append(vr_s)
    return (yp, ys, jnp.stack(a_p), jnp.stack(c_p), jnp.stack(a_s), jnp.stack(c_s), jnp.stack(v_s))
```

```python
import numpy as np
from contextlib import ExitStack
import concourse.bass as bass
import concourse.mybir as mybir
from concourse.bass_utils import run_bass_kernel_spmd

F32 = mybir.dt.float32
BF16 = mybir.dt.bfloat16
F32R = mybir.dt.float32r
AF = mybir.ActivationFunctionType
ALU = mybir.AluOpType

NCORES = 8
L = 2
D = 1024
KC = 8
DFF = 2816
JC = 22
SEQ = 2048
NS = 16
TT = 1024
NT = TT + NS
EPS = 1e-6
NVL = 146
NSLOT = 4
SLOT = 4096
NSCR = 6
STAT_F32R = False
SAME_ENG_SYNC = True
ONE_STREAM = False
EARLY_PARTB = True
MARKS = []


def wplan():
    plan = []
    for f in (1, 2):
        if f == 2:
            pass
        ups = [((f"f{f}u", jj), 4096) for jj in range(11)]
        downs = [((f"f{f}d", m), 2816) for m in range(8)]
        if f == 1:
            plan += ups + downs
            plan += [(("inA", 0), 3072), (("inA", 1), 3072), (("inU", 0), 4096), (("inV", 0), 4096),
                     (("inC", 0), 4096), (("out", 0), 4096), (("out", 1), 4096)]
        else:
            plan += ups + downs
    plan += [(("ewp", 0), 2048), (("ewg", 0), 4096), (("ewg", 1), 4096)]
    return plan


PLAN = wplan()
WL = sum(x for _, x in PLAN)
WOFF = {}
_o = 0
for _k, _x in PLAN:
    WOFF[_k] = (_o, _x)
    _o += _x
WTOT = L * WL


def _blk(Wm, cols):
    K = Wm.shape[0]
    sub = Wm[:, cols]
    return sub.reshape(K // 128, 128, sub.shape[1]).transpose(1, 0, 2)


def pack_weights(inp):
    out = np.empty((128, WTOT), np.float32)
    for l in range(L):
        for key, X in PLAN:
            off = l * WL + WOFF[key][0]
            kind, i = key
            if kind in ("f1u", "f2u"):
                f = kind[1]
                c = np.arange(256 * i, 256 * i + 256)
                a = np.stack([_blk(inp[f"f{f}_wg"][l], c), _blk(inp[f"f{f}_wu"][l], c)], axis=1)
            elif kind in ("f1d", "f2d"):
                f = kind[1]
                a = _blk(inp[f"f{f}_wd"][l], np.arange(128 * i, 128 * i + 128))
            elif kind == "inA":
                c = np.concatenate([np.arange(b + 128 * i, b + 128 * i + 128) for b in (0, 256, 512)])
                a = _blk(inp["w_in"][l], c)
            elif kind == "inU":
                a = _blk(inp["w_in"][l], np.arange(768, 1280))
            elif kind == "inV":
                a = _blk(inp["w_in"][l], np.arange(1280, 1792))
            elif kind == "inC":
                a = _blk(inp["w_in"][l], np.arange(1792, 2304))
            elif kind == "out":
                a = _blk(inp["w_out"][l], np.arange(512 * i, 512 * i + 512))
            elif kind == "ewp":
                a = _blk(inp["e_wp"][l], np.arange(0, 1024))
            elif kind == "ewg":
                a = _blk(inp["e_wg"][l], np.arange(512 * i, 512 * i + 512))
            out[:, off:off + X] = a.reshape(128, X)
    return out


NORM_NAMES = ["f1_pre", "f1_post", "m_pre", "m_post", "f2_pre", "f2_post", "e_pre", "e_post"]
G_F1PRE, G_F1POST, G_MPRE, G_MPOST, G_F2PRE, G_F2POST, G_EPRE, G_EPOST = [8 * i for i in range(8)]
V_ACW = 64
V_CCW = 70
V_CCB = 132
V_CLG = 134
V_CLB = 136
V_BLG = 138
V_BLB = 142


def pack_vec(inp):
    out = np.empty((128, L * NVL), np.float32)
    for l in range(L):
        o = l * NVL
        for i, nm in enumerate(NORM_NAMES):
            out[:, o + 8 * i:o + 8 * i + 8] = inp[nm][l].reshape(8, 128).T
        out[:, o + V_ACW:o + V_ACW + 6] = inp["a_conv_w"][l].reshape(3, 2, 128).transpose(2, 1, 0).reshape(128, 6)
        out[:, o + V_CCW:o + V_CCW + 62] = inp["c_conv_w"][l].reshape(31, 2, 128).transpose(2, 1, 0).reshape(128, 62)
        out[:, o + V_CCB:o + V_CCB + 2] = inp["c_conv_b"][l].reshape(2, 128).T
        out[:, o + V_CLG:o + V_CLG + 2] = inp["c_ln_g"][l].reshape(2, 128).T
        out[:, o + V_CLB:o + V_CLB + 2] = inp["c_ln_b"][l].reshape(2, 128).T
        out[:, o + V_BLG:o + V_BLG + 4] = inp["b_ln_g"][l].T
        out[:, o + V_BLB:o + V_BLB + 4] = inp["b_ln_b"][l].T
    return out


class Op:
    __slots__ = ("eng", "fn", "deps", "signal", "chan", "val")


class Sched:
    ENGS = ("pe", "act", "dve", "pool", "sp")

    def __init__(self):
        self.ops = {e: [] for e in self.ENGS}
        self.lastw = {}
        self.readers = {}
        self.chan_cnt = {}
        self.chan_last = {}

    def add(self, eng, fn, r=(), w=(), chan=None):
        op = Op()
        op.eng, op.fn, op.chan, op.signal, op.val = eng, fn, chan, False, None
        deps = []
        if chan is not None:
            if chan in self.chan_last:
                deps.append(self.chan_last[chan])
            self.chan_last[chan] = op
        for x in r:
            deps.extend(self.lastw.get(x, ()))
        for x in w:
            deps.extend(self.lastw.get(x, ()))
            deps.extend(self.readers.get(x, ()))
        seen = set()
        keep = []
        for d in deps:
            if id(d) in seen:
                continue
            seen.add(id(d))
            if d.chan is None and chan is None and d.eng == eng and (eng == "pe" or not SAME_ENG_SYNC):
                continue
            keep.append(d)
        op.deps = keep
        if chan is not None:
            c = self.chan_cnt.get(chan, 0) + 16
            self.chan_cnt[chan] = c
            op.val = c
        ws = set(w)
        for x in w:
            self.lastw[x] = [op]
            self.readers[x] = []
        for x in r:
            if x in ws:
                continue
            lst = self.readers.setdefault(x, [])
            if chan is None:
                lst[:] = [o for o in lst if not (o.chan is None and o.eng == eng)]
            lst.append(op)
        self.ops[eng].append(op)
        return op

    def transfer(self, old, new):
        ws, rs = [], []
        sw, sr = set(), set()
        for x in old:
            for o in self.lastw.get(x, ()):
                if id(o) not in sw:
                    sw.add(id(o))
                    ws.append(o)
            for o in self.readers.get(x, ()):
                if id(o) not in sr:
                    sr.add(id(o))
                    rs.append(o)
        def compact(lst):
            last = {}
            out = []
            for o in lst:
                if o.chan is None:
                    last[o.eng] = o
                else:
                    out.append(o)
            return out + list(last.values())
        ws = compact(ws)
        rs = compact(rs)
        for y in new:
            self.lastw[y] = list(ws)
            self.readers[y] = list(rs)

    def finalize(self):
        for lst in self.ops.values():
            for op in lst:
                for d in op.deps:
                    if d.chan is None:
                        d.signal = True
        for lst in self.ops.values():
            c = 0
            for op in lst:
                if op.chan is None:
                    if op.signal:
                        c += 1
                    op.val = c

    def emit(self, name, eng, sems, final_waits=()):
        waited = {}
        for op in self.ops[name]:
            need = {}
            for d in op.deps:
                key = d.chan if d.chan is not None else d.eng
                if d.val > need.get(key, 0):
                    need[key] = d.val
            for key, v in need.items():
                if waited.get(key, 0) < v:
                    eng.wait_ge(sems[key], v)
                    waited[key] = v
            ins = op.fn(eng)
            if op.chan is not None:
                ins.then_inc(sems[op.chan], 16)
            elif op.signal:
                ins.then_inc(sems[op.eng], 1)
        for key in final_waits:
            if self.chan_cnt.get(key, 0) > 0:
                eng.wait_ge(sems[key], self.chan_cnt[key])


def build_program():
    nc = bass.Bass("TRN2", target_bir_lowering=False)
    S = Sched()

    def din(name, shape):
        return nc.dram_tensor(name, list(shape), F32, kind="ExternalInput").ap()

    def dout(name, shape):
        return nc.dram_tensor(name, list(shape), F32, kind="ExternalOutput").ap()

    xT = din("xT", [D, SEQ])
    xsT = din("xsT", [D, NS])
    pT = din("pT", [L, 256, SEQ])
    psT = din("psT", [L, 256, NS])
    sa = din("sa", [L, NS, 2, 256])
    saT = din("saT", [L, 256, 2, NS])
    sc = din("sc", [L, NS, 30, 256])
    scT = din("scT", [L, 256, 30, NS])
    Wd = din("W", [128, WTOT])
    VECd = din("VEC", [128, L * NVL])
    ROWS = din("ROWS", [L, 3, 512])
    WSTd = din("WST", [L, 128, 512])
    WS00 = din("WS00", [1, L * 4])
    CONSTS = din("CONSTS", [128, 256])

    yT = dout("yT", [D, SEQ])
    ysT = dout("ysT", [D, NS])
    cap = dout("cap", [L, 256, 2])
    ccp = dout("ccp", [L, 256, 30])
    cash = dout("cash", [L, NS, 1, 256])
    casn = dout("casn", [L, 256, NS])
    ccsh = dout("ccsh", [L, NS, 29, 256])
    ccsn = dout("ccsn", [L, 256, NS])
    vs = dout("vs", [L, NS, 512])

    es = ExitStack()

    def sb(name, shape, dt):
        return es.enter_context(nc.sbuf_tensor(name, list(shape), dt))

    XTt = sb("XT", [128, KC * NT], F32)
    XT = XTt[:].rearrange("p (k t) -> p k t", k=KC)
    R1t = sb("R1", [128, KC * NT], F32)
    YSB = R1t[:].rearrange("p (k t) -> p k t", k=KC)
    H = R1t[:, 0:KC * NT // 2].bitcast(BF16).rearrange("p (k t) -> p k t", k=KC)
    HIDt = sb("HIDR", [128, JC * NT], BF16)
    Vt = sb("V", [128, 9 * 512], BF16)
    V = Vt[:].rearrange("p (g f) -> p g f", g=9)
    RING = [sb(f"RING{i}", [128, SLOT], BF16) for i in range(NSLOT)]
    SQt = sb("SQ", [128, 1024], F32)
    SQS = [sb(f"SQS{i}", [128, 512], BF16) for i in range(4)]
    VECH = sb("VECH", [128, L * 16], F32)
    SDT = F32R if STAT_F32R else BF16
    if STAT_F32R:
        raise NotImplementedError
    else:
        CBR = SQt[:, 0:512].bitcast(BF16).rearrange("p (k t) -> p k t", k=2)
        CSQ = SQt[:, 512:1024].bitcast(BF16).rearrange("p (k t) -> p k t", k=2)
    SCR = [sb(f"SCR{i}", [128, 512], F32) for i in range(NSCR)]
    CBt = sb("CB", [128, 1024], F32)
    CB = CBt[:].rearrange("p (k t) -> p k t", k=2)
    DIAGAt = sb("DIAGA", [128, 6 * 128], BF16)
    DIAGA = DIAGAt[:].rearrange("p (c k m) -> p c k m", c=2, k=3)
    DIAGCt = sb("DIAGC", [128, 62 * 128], BF16)
    DIAGC = DIAGCt[:].rearrange("p (c k m) -> p c k m", c=2, k=31)
    PTt = sb("PT", [128, 2 * NT], BF16)
    PT = PTt[:].rearrange("p (k t) -> p k t", k=2)
    CST = sb("CST", [128, 256], F32)
    IDENT = CST[:, 0:128]
    MASK = CST[:, 128:256]
    ONESB = sb("ONESB", [128, 128], BF16)
    ONES256 = sb("ONES256", [128, 128], SDT)
    CHt = sb("CH", [128, 512], F32)
    CH = CHt[:].rearrange("p (h i) -> p h i", h=4)
    CHSt = sb("CHS", [128, 64], F32)
    CHS = CHSt[:].rearrange("p (h i) -> p h i", h=4)
    S1 = sb("S1", [128, 4], F32)
    ONES16 = sb("ONES16", [128, 16], F32)
    VEt = sb("VE", [128, 8], F32)
    ITt = sb("IT", [128, 8], mybir.dt.int32)
    T4t = sb("T4", [128, 8], F32)
    VECt = sb("VECT", [128, L * NVL], F32)
    LNG = CBt[:, 0:512]
    LNB = CBt[:, 512:1024]
    WSTBt = sb("WSTB", [128, L * 512], BF16)
    WSTB = WSTBt[:].rearrange("p (l h i) -> p l h i", l=L, h=4)
    WSSt = sb("WSS", [16, L * 64], BF16)
    WSS = WSSt[:].rearrange("p (l h i) -> p l h i", l=L, h=4)
    W00T = sb("W00T", [128, L * 4], F32)
    FAHt = sb("FAH", [128, L * 4], BF16)
    FAH = FAHt[:].rearrange("p (l c k) -> p l c k", l=L, c=2)
    FCHt = sb("FCH", [128, L * 60], BF16)
    FCH = FCHt[:].rearrange("p (l c k) -> p l c k", l=L, c=2)
    FASt = sb("FAS", [128, 2 * 3 * NS], BF16)
    FAS = FASt[:].rearrange("p (c k s) -> p c k s", c=2, k=3)
    FCSt = sb("FCS", [128, 2 * 31 * NS], BF16)
    FCS = FCSt[:].rearrange("p (c k s) -> p c k s", c=2, k=31)
    CAOt = sb("CAO", [128, L * 4], F32)
    CAO = CAOt[:].rearrange("p (l c k) -> p l c k", l=L, c=2)
    CCOt = sb("CCO", [128, L * 60], F32)
    CCO = CCOt[:].rearrange("p (l c k) -> p l c k", l=L, c=2)
    CASNt = sb("CASN", [128, L * 2 * NS], F32)
    CASN = CASNt[:].rearrange("p (l c s) -> p l c s", l=L, c=2)
    CCSNt = sb("CCSN", [128, L * 2 * NS], F32)
    CCSN = CCSNt[:].rearrange("p (l c s) -> p l c s", l=L, c=2)
    STATSt = sb("STATS", [128, 48], F32)
    STATS2 = STATSt[:].rearrange("p (b h s) -> p b h s", b=2, h=4)
    MVt = sb("MV", [128, 16], F32)
    MV2 = MVt[:].rearrange("p (b h s) -> p b h s", b=2, h=4)
    RSt = sb("RS", [128, 8], F32)

    PS = [es.enter_context(nc.psum_tensor(f"PS{i}", [128, 512], F32)) for i in range(8)]

    def mark(label):
        MARKS.append((label, len(S.ops["pe"])))

    def MM(out, lhsT, rhs, start, stop, r, w):
        S.add("pe", lambda e: e.matmul(out, lhsT, rhs, start=start, stop=stop), r=r, w=w)

    def ACT(out, in_, func, r, w, bias=None, scale=None):
        kw = {}
        if bias is not None:
            kw["bias"] = bias
        if scale is not None:
            kw["scale"] = scale
        S.add("act", lambda e: e.activation(out=out, in_=in_, func=func, **kw), r=r, w=w)

    def TT_(out, in0, in1, op, r, w, eng="dve"):
        S.add(eng, lambda e: e.tensor_tensor(out=out, in0=in0, in1=in1, op=op), r=r, w=w)

    def STT(out, in0, scalar, in1, op0, op1, r, w):
        S.add("dve", lambda e: e.scalar_tensor_tensor(out=out, in0=in0, scalar=scalar, in1=in1, op0=op0, op1=op1), r=r, w=w)

    def TS(out, in0, s1, s2, op0, op1, r, w, eng="dve"):
        if s2 is None:
            S.add(eng, lambda e: e.tensor_scalar(out=out, in0=in0, scalar1=s1, scalar2=None, op0=op0), r=r, w=w)
        else:
            S.add(eng, lambda e: e.tensor_scalar(out=out, in0=in0, scalar1=s1, scalar2=s2, op0=op0, op1=op1), r=r, w=w)

    def MEMSET(ap, val, w):
        S.add("dve", lambda e: e.memset(ap, val), r=(), w=w)

    chan_rot = {"c:out": [0, 4], "c:const": [0, 8], "c:lay": [0, 5], "c:xin": [0, 3], "c:pin": [0, 3]}

    def DMA(q, out, in_, r, w, chan):
        if chan in chan_rot:
            st = chan_rot[chan]
            chan = f"{chan}{st[0] % st[1]}"
            st[0] += 1
        S.add(q, lambda e: e.dma_start(out=out, in_=in_), r=r, w=w, chan=chan)

    bank_i = {"mm": 0, "st": 0}
    st_held = set()

    def bank(cls, hold=False):
        if cls == "mm":
            b = bank_i["mm"] % 4
            bank_i["mm"] += 1
            return PS[b], f"PS{b}"
        for _ in range(4):
            b = 4 + bank_i["st"] % 4
            bank_i["st"] += 1
            if b not in st_held:
                if hold:
                    st_held.add(b)
                return PS[b], f"PS{b}"
        raise RuntimeError("no free statistics bank")

    def bank_release(name):
        st_held.discard(int(name[2:]))

    scr_i = [0]

    def scr():
        i = scr_i[0] % NSCR
        scr_i[0] += 1
        return SCR[i], f"SCR{i}"

    sqs_i = [0, 0]

    def sqs(stream=0):
        i = 2 * stream + sqs_i[stream] % 2
        sqs_i[stream] += 1
        return SQS[i], f"SQS{i}"

    WSEQ = [(l, key) for _ti in range(2) for l in range(L) for key, _x in PLAN]
    wst = {"cursor": 0, "next": 0, "released": set()}

    def w_try_issue():
        while wst["next"] < len(WSEQ) and (wst["next"] < NSLOT or (wst["next"] - NSLOT) in wst["released"]):
            i = wst["next"]
            l, key = WSEQ[i]
            off, X = WOFF[key]
            off += l * WL
            sl = i % NSLOT
            DMA("pool", RING[sl][:, 0:X], Wd[:, off:off + X], r=(), w=[f"RING{sl}"], chan=f"c:ring{sl}")
            wst["next"] += 1

    def wload(l, key):
        i = wst["cursor"]
        assert WSEQ[i] == (l, key), (WSEQ[i], l, key)
        assert i < wst["next"]
        wst["cursor"] += 1
        sl = i % NSLOT
        return RING[sl], f"RING{sl}", i

    def wrelease(i):
        wst["released"].add(i)
        w_try_issue()

    def vec(l, col, n=1):
        return VECt[:, l * NVL + col:l * NVL + col + n]

    DMA("sp", CST[:], CONSTS, r=(), w=["CST"], chan="c:const")
    DMA("sp", VECt[:], VECd, r=(), w=["VEC"], chan="c:const")
    DMA("sp", W00T[:], WS00.partition_broadcast(128), r=(), w=["W00T"], chan="c:const")
    w_try_issue()
    MEMSET(ONESB[:], 1.0 / D, ["ONESB"])
    MEMSET(ONES256[:], 1.0 / 256.0, ["ONES256"])
    MEMSET(ONES16[:], 1.0, ["ONES16"])
    for l in range(L):
        TS(VECH[:, l * 16:l * 16 + 8], vec(l, G_F1POST, 8), 0.5, None, ALU.mult, None, r=["VEC"], w=["VECH"])
        TS(VECH[:, l * 16 + 8:l * 16 + 16], vec(l, G_F2POST, 8), 0.5, None, ALU.mult, None, r=["VEC"], w=["VECH"])
        t, tn = scr()
        DMA("sp", t[:], WSTd[l], r=(), w=[tn], chan="c:const")
        for hd in range(4):
            TT_(WSTB[:, l, hd, :], t[:, hd * 128:(hd + 1) * 128], MASK, ALU.mult, r=[tn, "CST"], w=["WSTB"])
            TS(WSS[:, l, hd, :], CST[0:16, 0:16], W00T[0:16, l * 4 + hd:l * 4 + hd + 1], None, ALU.mult, None,
               r=["CST", "W00T"], w=["WSS"])

    TILES = [
        [(0, 384 + NS, 0, 384, NS), (384 + NS, 384, 384, 384, 0), (768 + NS, 256, 768, 256, 0)],
        [(0, 512, 0, 512, 0), (512, 512, 512, 512, 0)],
    ]

    def nm(prefix, si, cnt):
        return [f"{prefix}.{si}.{k}" for k in range(cnt)]

    def allnames(prefix, nsub, cnt):
        out = []
        for si in range(nsub):
            out += nm(prefix, si, cnt)
        return out

    out_chans = [f"c:out{i}" for i in range(4)]

    for ti, subs in enumerate(TILES):
        nsub = len(subs)
        has_s = any(sb_[4] for sb_ in subs)
        n0_, n1_ = subs[0][1], subs[1][1]
        FAX = HIDt[:, 14 * n0_:14 * n0_ + 2 * (TT + 2)].rearrange("p (k t) -> p k t", k=2)
        fcx0 = JC * n0_ + 14 * n1_
        FCX = HIDt[:, fcx0:fcx0 + 2 * (TT + 30)].rearrange("p (k t) -> p k t", k=2)
        assert 2 * (TT + 2) <= 8 * n0_ and 2 * (TT + 30) <= 8 * n1_
        tok0 = ti * TT
        last_tile = ti == len(TILES) - 1
        Hn = allnames("H", nsub, 8)
        YSBn = allnames("YSB", nsub, 8)
        HIDn = allnames("HID", nsub, JC)
        YSB2n = allnames("YSB2", nsub, 8)
        MIXERn = (allnames("MIX", nsub, 8) + allnames("U", nsub, 4) + allnames("AB", nsub, 2)
                  + [f"FAX.{c}.{x}" for c in range(2) for x in ["h"] + list(range(nsub))]
                  + [f"FCX.{c}.{x}" for c in range(2) for x in ["h"] + list(range(nsub))])

        xv = xT.rearrange("(k p) t -> p k t", p=128)
        def colmap():
            pieces = []
            for si, (c0, n, t0_, npr, nsa) in enumerate(subs):
                if pieces and pieces[-1][0] == "p" and pieces[-1][2] == c0 and pieces[-1][4] == t0_:
                    k_, a_, b_, ta_, tb_, sis_ = pieces[-1]
                    pieces[-1] = ("p", a_, c0 + npr, ta_, t0_ + npr, sis_ + [si])
                else:
                    pieces.append(("p", c0, c0 + npr, t0_, t0_ + npr, [si]))
                if nsa:
                    pieces.append(("s", c0 + npr, c0 + n, 0, nsa, [si]))
            return pieces

        for (k_, a_, b_, ta_, tb_, sis_) in colmap():
            nms = [x for si_ in sis_ for x in nm("XT", si_, 8)]
            if k_ == "p":
                DMA("sp", XT[:, :, a_:b_], xv[:, :, tok0 + ta_:tok0 + tb_], r=(), w=nms, chan="c:xin")
            else:
                DMA("sp", XT[:, :, a_:b_], xsT.rearrange("(k p) t -> p k t", p=128), r=(), w=nms, chan="c:xin")

        streams = [list(range(nsub)), []] if ONE_STREAM else [[0], list(range(1, nsub))]

        def Hs(si, k, a=0, b=None):
            c0, n = subs[si][0], subs[si][1]
            hv = R1t[:, 8 * c0:8 * c0 + 4 * n].bitcast(BF16)
            return hv[:, k * n + a:k * n + (n if b is None else b)]

        def Ys(si, m):
            c0, n = subs[si][0], subs[si][1]
            return R1t[:, 8 * c0 + m * n:8 * c0 + (m + 1) * n]

        def Ys2(si, m):
            c0, n = subs[si][0], subs[si][1]
            yv = HIDt[:, JC * c0:JC * c0 + 16 * n].bitcast(F32)
            return yv[:, m * n:(m + 1) * n]

        def HDs(si, j):
            c0, n = subs[si][0], subs[si][1]
            return HIDt[:, JC * c0 + j * n:JC * c0 + (j + 1) * n]
        pend = {"gen": None, "partB": None}

        def step(gen, k):
            if gen is None:
                return
            for _ in range(k):
                try:
                    next(gen)
                except StopIteration:
                    return

        tick_state = {"gen": None, "per": 0}

        def tick():
            if tick_state["gen"] is not None:
                step(tick_state["gen"], tick_state["per"])

        def gen_steps(sis):
            return 1 + KC * len(sis)

        def resolve_pending():
            if pend["partB"] is not None:
                step(pend["gen"], 10 ** 6)
                pend["partB"]()
                pend["gen"] = pend["partB"] = None

        def rstd_inplace(st, stn, n):
            t, tn = scr()
            ACT(t[:, 0:n], st[:, 0:n], AF.Ln, r=[stn], w=[tn], bias=EPS, scale=1.0)
            ACT(st[:, 0:n], t[:, 0:n], AF.Exp, r=[tn], w=[stn], scale=-0.5)

        def h_apply_sub(l2, gcol2, si, st2, st2n):
            c0, n = subs[si][0], subs[si][1]
            for k in range(KC):
                STT(Hs(si, k), XT[:, k, c0:c0 + n], vec(l2, gcol2 + k), st2[:, 0:n], ALU.mult, ALU.mult,
                    r=[f"XT.{si}.{k}", st2n, "VEC"], w=[f"H.{si}.{k}"])

        def prenorm_first(l2, gcol2):
            for si, (c0, n, t0_, npr, nsa) in enumerate(subs):
                st2, st2n = bank("st")
                for k in range(KC):
                    sq, sqn = sqs()
                    ACT(sq[:, 0:n], XT[:, k, c0:c0 + n], AF.Square, r=[f"XT.{si}.{k}"], w=[sqn])
                    MM(st2[:, 0:n], ONESB[:], sq[:, 0:n], k == 0, k == KC - 1, r=[sqn, "ONESB"], w=[st2n])
                rstd_inplace(st2, st2n, n)
                h_apply_sub(l2, gcol2, si, st2, st2n)

        class YStats:
            def __init__(self, sis):
                self.stream = 0 if (len(sis) and sis[0] == 0) else 1
                self.sts = {}
                self.pending = None
                self.cnt = {si: 0 for si in sis}

            def flush(self):
                if self.pending is not None:
                    si, sq, sqn, n = self.pending
                    st, stn = self.sts[si]
                    MM(st[:, 0:n], ONESB[:], sq[:, 0:n], self.cnt[si] == 0, self.cnt[si] == KC - 1, r=[sqn, "ONESB"], w=[stn])
                    self.cnt[si] += 1
                    self.pending = None

            def add(self, si, src_ap, src_res, n):
                if si not in self.sts:
                    self.sts[si] = bank("st", hold=True)
                sq, sqn = sqs(self.stream)
                ACT(sq[:, 0:n], src_ap, AF.Square, r=src_res, w=[sqn])
                self.flush()
                self.pending = (si, sq, sqn, n)

        class Bnd:
            def __init__(self, sis, gains, Y, yname, ys, nxt):
                self.sis, self.gains, self.Y, self.yname, self.ys, self.nxt = sis, gains, Y, yname, ys, nxt

            def partA(self):
                for si in self.sis:
                    st, stn = self.ys.sts[si]
                    rstd_inplace(st, stn, subs[si][1])
                yield
                LAG = 2
                for si in self.sis:
                    c0, n = subs[si][0], subs[si][1]
                    st, stn = self.ys.sts[si]

                    def square(m, si=si, c0=c0, n=n):
                        ww = [f"H.{si}.{m}"] + ([f"YSB.{si}.{m // 2}"] if self.yname == "YSB" else [])
                        ACT(Hs(si, m), XT[:, m, c0:c0 + n], AF.Square, r=[f"XT.{si}.{m}"], w=ww)

                    for m in range(KC + LAG):
                        if m < KC:
                            t, tn = scr()
                            STT(t[:, 0:n], self.Y(si, m), self.gains[:, m:m + 1], st[:, 0:n], ALU.mult, ALU.mult,
                                r=[f"{self.yname}.{si}.{m}", stn, "VEC", "VECH"], w=[tn])
                            TT_(XT[:, m, c0:c0 + n], XT[:, m, c0:c0 + n], t[:, 0:n], ALU.add,
                                r=[tn, f"XT.{si}.{m}"], w=[f"XT.{si}.{m}"], eng="pool")
                        if self.nxt is not None and m >= LAG:
                            square(m - LAG)
                        if m < KC:
                            yield
                    bank_release(stn)

            def partB(self):
                if self.nxt is None:
                    return
                for si in self.sis:
                    c0, n = subs[si][0], subs[si][1]
                    st2, st2n = bank("st")
                    for m in range(KC):
                        MM(st2[:, 0:n], ONESB[:], Hs(si, m), m == 0, m == KC - 1, r=[f"H.{si}.{m}", "ONESB"], w=[st2n])
                    rstd_inplace(st2, st2n, n)
                    h_apply_sub(self.nxt[0], self.nxt[1], si, st2, st2n)

        def run_phase(l, keys, work, head=0, tail=0, ys=None, mk_bnd=None, gpc=1, parts=1):
            nk = len(keys)
            if head + tail >= nk:
                blocks = [("HT", list(range(nk)))]
            else:
                blocks = []
                if head:
                    blocks.append(("H", list(range(head))))
                blocks.append(("M", list(range(head, nk - tail))))
                if tail:
                    blocks.append(("T", list(range(nk - tail, nk))))
            for kind, idxs in blocks:
                if kind == "M":
                    for p in idxs:
                        slot, sn, wi = wload(l, keys[p])
                        for stm in streams:
                            for si in stm:
                                work(p, si, slot, sn)
                        wrelease(wi)
                    continue
                is_head = "H" in kind
                is_tail = ("T" in kind) and (mk_bnd is not None)
                loaded = [(p,) + wload(l, keys[p]) for p in idxs]
                if is_head and pend["gen"] is not None:
                    ngrp = max(1, len(loaded) * len(streams[0]) * gpc)
                    tick_state["gen"] = pend["gen"]
                    tick_state["per"] = -(-gen_steps(streams[1]) // ngrp)
                acalls = [(p, si, slot, sn, pt) for (p, slot, sn, wi) in loaded for si in streams[0] for pt in range(parts)]
                for ci, (p, si, slot, sn, pt) in enumerate(acalls):
                    if is_head and EARLY_PARTB and len(acalls) >= 3 and ci == len(acalls) - 1:
                        tick_state["gen"] = None
                        resolve_pending()
                    if parts > 1:
                        work(p, si, slot, sn, part=pt)
                    else:
                        work(p, si, slot, sn)
                tick_state["gen"] = None
                if is_head:
                    resolve_pending()
                genA = None
                bA = mk_bnd(streams[0], ys[0]) if is_tail else None
                if is_tail:
                    ys[0].flush()
                    genA = bA.partA()
                    step(genA, 1)
                    ngrp = max(1, len(loaded) * len(streams[1]) * gpc)
                    tick_state["gen"] = genA
                    tick_state["per"] = -(-gen_steps(streams[0]) // ngrp)
                calls = [(p, si, slot, sn, pt) for (p, slot, sn, wi) in loaded for si in streams[1] for pt in range(parts)]
                partb_done = False
                for ci, (p, si, slot, sn, pt) in enumerate(calls):
                    if is_tail and EARLY_PARTB and len(calls) >= 3 and ci == len(calls) - 1:
                        tick_state["gen"] = None
                        step(genA, 10 ** 6)
                        bA.partB()
                        partb_done = True
                    if parts > 1:
                        work(p, si, slot, sn, part=pt)
                    else:
                        work(p, si, slot, sn)
                tick_state["gen"] = None
                if is_tail:
                    if not partb_done:
                        step(genA, 10 ** 6)
                        bA.partB()
                    ys[1].flush()
                    bB = mk_bnd(streams[1], ys[1])
                    pend["gen"] = bB.partA()
                    step(pend["gen"], 1)
                    pend["partB"] = bB.partB
                for (p, slot, sn, wi) in loaded:
                    wrelease(wi)

        def ffn(l, f, gains_post, nxt):
            def up_work(jj, si, slot, sn):
                c0, n = subs[si][0], subs[si][1]
                wv = slot[:, 0:4096].rearrange("p (a k c) -> p a k c", a=2, k=KC)
                for jl in range(2):
                    j = 2 * jj + jl
                    gp, gpn = bank("mm")
                    up, upn = bank("mm")
                    for k in range(KC):
                        MM(gp[:, 0:n], wv[:, 0, k, jl * 128:(jl + 1) * 128], Hs(si, k), k == 0, k == KC - 1,
                           r=[sn, f"H.{si}.{k}"], w=[gpn])
                    for k in range(KC):
                        MM(up[:, 0:n], wv[:, 1, k, jl * 128:(jl + 1) * 128], Hs(si, k), k == 0, k == KC - 1,
                           r=[sn, f"H.{si}.{k}"], w=[upn])
                    t, tn = scr()
                    ACT(t[:, 0:n], gp[:, 0:n], AF.Silu, r=[gpn], w=[tn])
                    TT_(HDs(si, j), t[:, 0:n], up[:, 0:n], ALU.mult, r=[tn, upn], w=[f"HID.{si}.{j}"])
                    tick()

            run_phase(l, [(f"f{f}u", jj) for jj in range(11)], up_work, head=3, tail=0, gpc=2)
            mark(f"t{ti}l{l}.f{f}down")
            if f == 1:
                for ch in range(2):
                    for k in range(3):
                        TS(DIAGA[:, ch, k, :], IDENT, vec(l, V_ACW + ch * 3 + k), None, ALU.mult, None,
                           r=["CST", "VEC"], w=[f"DIAGA.{ch}"])
                    for k in range(31):
                        TS(DIAGC[:, ch, k, :], IDENT, vec(l, V_CCW + ch * 31 + k), None, ALU.mult, None,
                           r=["CST", "VEC"], w=[f"DIAGC.{ch}"])
            S.transfer(Hn, YSBn)
            ys = (YStats(streams[0]), YStats(streams[1]))

            def down_work(m, si, slot, sn):
                c0, n = subs[si][0], subs[si][1]
                wv = slot[:, 0:2816].rearrange("p (j c) -> p j c", j=JC)
                yp, ypn = bank("mm")
                for j in range(JC):
                    MM(yp[:, 0:n], wv[:, j, :], HDs(si, j), j == 0, j == JC - 1,
                       r=[sn, f"HID.{si}.{j}"], w=[ypn])
                ACT(Ys(si, m), yp[:, 0:n], AF.Copy, r=[ypn], w=[f"YSB.{si}.{m}"])
                ys[0 if si in streams[0] else 1].add(si, yp[:, 0:n], [ypn], n)
                tick()

            run_phase(l, [(f"f{f}d", m) for m in range(KC)], down_work, head=0, tail=3, ys=ys,
                      mk_bnd=lambda sis, y: Bnd(sis, gains_post, Ys, "YSB", y, nxt))

        prenorm_first(0, G_F1PRE)

        for l in range(L):
            if has_s:
                DMA("sp", CBt[:, 0:960].rearrange("p (c x) -> p c x", c=2),
                    scT[l].rearrange("(c p) k s -> p c (k s)", p=128), r=(), w=["CB.0", "CB.1"], chan="c:lay")
                DMA("sp", CBt[:, 960:1024].rearrange("p (c x) -> p c x", c=2),
                    saT[l].rearrange("(c p) k s -> p c (k s)", p=128), r=(), w=["CB.1"], chan="c:lay")
                S.add("act", lambda e: e.activation(out=FCS[:, :, 0:30, :], in_=CBt[:, 0:960].rearrange("p (c k s) -> p c k s", c=2, k=30), func=AF.Copy),
                      r=["CB.0", "CB.1"], w=["FCS.h"])
                S.add("act", lambda e: e.activation(out=FAS[:, :, 0:2, :], in_=CBt[:, 960:1024].rearrange("p (c k s) -> p c k s", c=2, k=2), func=AF.Copy),
                      r=["CB.1"], w=["FAS.h"])
                DMA("sp", cash[l], sa[l][:, 1:2, :], r=(), w=(), chan="c:out")
                DMA("sp", ccsh[l], sc[l][:, 1:30, :], r=(), w=(), chan="c:out")
                DMA("sp", LNG[0:NS, :], ROWS[l, 0:1, :].partition_broadcast(NS), r=(), w=["CB.0"], chan="c:lay")
                DMA("sp", LNB[0:NS, :], ROWS[l, 1:2, :].partition_broadcast(NS), r=(), w=["CB.1"], chan="c:lay")

            mark(f"t{ti}l{l}.ffn1")
            ffn(l, 1, VECH[:, l * 16:l * 16 + 8], (l, G_MPRE))

            mark(f"t{ti}l{l}.mixA")
            S.transfer(HIDn, MIXERn)
            pv = pT[l].rearrange("(k p) t -> p k t", p=128)
            for (k_, a_, b_, ta_, tb_, sis_) in colmap():
                if k_ == "p":
                    DMA("pool", PT[:, :, a_:b_], pv[:, :, tok0 + ta_:tok0 + tb_], r=(), w=["PT"], chan="c:pin")
                else:
                    DMA("pool", PT[:, :, a_:b_], psT[l].rearrange("(k p) t -> p k t", p=128), r=(), w=["PT"], chan="c:pin")
            if ti == 0:
                MEMSET(FAX[:, :, 0:2], 0.0, ["FAX.0.h", "FAX.1.h"])
                MEMSET(FCX[:, :, 0:30], 0.0, ["FCX.0.h", "FCX.1.h"])
            else:
                ACT(FAX[:, :, 0:2], FAH[:, l, :, :], AF.Copy, r=["FAH"], w=["FAX.0.h", "FAX.1.h"])
                ACT(FCX[:, :, 0:30], FCH[:, l, :, :], AF.Copy, r=["FCH"], w=["FCX.0.h", "FCX.1.h"])

            bbc, bbcn = scr()
            DMA("sp", bbc[:], ROWS[l, 2:3, :].partition_broadcast(128), r=(), w=[bbcn], chan="c:lay")
            prs, prsn = bank("mm")
            MM(prs[:, :], ONESB[:], WSTBt[:, l * 512:(l + 1) * 512], True, True, r=["ONESB", "WSTB"], w=[prsn])
            rsum, rsumn = scr()
            ACT(rsum[:], prs[:], AF.Copy, r=[prsn], w=[rsumn], scale=float(D))
            for hd in range(4):
                hs = slice(hd * 128, (hd + 1) * 128)
                STT(CH[:, hd, :], rsum[:, hs], vec(l, V_BLB + hd), bbc[:, hs], ALU.mult, ALU.add,
                    r=[rsumn, bbcn, "VEC"], w=["CH"])
            if has_s:
                for hd in range(4):
                    STT(S1[:, hd:hd + 1], W00T[:, l * 4 + hd:l * 4 + hd + 1], vec(l, V_BLB + hd), bbc[:, hd * 128:hd * 128 + 1],
                        ALU.mult, ALU.add, r=["W00T", "VEC", bbcn], w=["S1"])
                    TS(CHS[:, hd, :], ONES16[:], S1[:, hd:hd + 1], None, ALU.mult, None, r=["ONES16", "S1"], w=["CHS"])

            def prev(prefix, ch, si):
                return f"{prefix}.{ch}.{'h' if si == 0 else si - 1}"

            def conv_a(ch, si):
                c0, n, t0_, npr, nsa = subs[si]
                cv, cvn = bank("mm")
                for k in range(3):
                    MM(cv[:, 0:npr], DIAGA[:, ch, k, :], FAX[:, ch, t0_ + k:t0_ + k + npr], k == 0, k == 2,
                       r=[f"FAX.{ch}.{si}", prev("FAX", ch, si), f"DIAGA.{ch}"], w=[cvn])
                if nsa:
                    for k in range(3):
                        MM(cv[:, npr:n], DIAGA[:, ch, k, :], FAS[:, ch, k, :], k == 0, k == 2,
                           r=["FAS.h", f"FAS.n.{ch}", f"DIAGA.{ch}"], w=[cvn])
                TT_(HDs(si, ch), HDs(si, 12 + ch), cv[:, 0:n], ALU.mult,
                    r=[f"AB.{si}.{ch}", cvn], w=[f"MIX.{si}.{ch}"])

            deferred = [None]

            def defer(fn):
                if deferred[0] is not None:
                    deferred[0]()
                deferred[0] = fn

            def a_work(ch, si, slot, sn):
                c0, n, t0_, npr, nsa = subs[si]
                wv = slot[:, 0:3072].rearrange("p (k c) -> p k c", k=KC)
                pv_, pvn = bank("mm")
                pc, pcn = bank("mm")
                pb, pbn = bank("mm")
                for (pp, ppn, co) in ((pv_, pvn, 0), (pc, pcn, 128), (pb, pbn, 256)):
                    for k in range(KC):
                        MM(pp[:, 0:n], wv[:, k, co:co + 128], Hs(si, k), k == 0, k == KC - 1,
                           r=[sn, f"H.{si}.{k}"], w=[ppn])
                av, avn = scr()
                ACT(av[:, 0:n], pv_[:, 0:n], AF.Copy, r=[pvn], w=[avn])
                TT_(FAX[:, ch, 2 + t0_:2 + t0_ + npr], av[:, 0:npr], pc[:, 0:npr], ALU.mult, r=[avn, pcn], w=[f"FAX.{ch}.{si}"])
                if last_tile and si == nsub - 1:
                    TT_(CAO[:, l, ch, :], av[:, npr - 2:npr], pc[:, npr - 2:npr], ALU.mult, r=[avn, pcn], w=[f"CAO.{l}"])
                if nsa:
                    TT_(FAS[:, ch, 2, :], av[:, npr:n], pc[:, npr:n], ALU.mult, r=[avn, pcn], w=[f"FAS.n.{ch}"])
                    TT_(CASN[:, l, ch, :], av[:, npr:n], pc[:, npr:n], ALU.mult, r=[avn, pcn], w=[f"CASN.{l}"])
                ACT(HDs(si, 12 + ch), pb[:, 0:n], AF.Copy, r=[pbn], w=[f"AB.{si}.{ch}"])
                defer(lambda ch=ch, si=si: conv_a(ch, si))
                tick()

            run_phase(l, [("inA", 0), ("inA", 1)], a_work, head=2, tail=0)
            defer(None)

            mark(f"t{ti}l{l}.mixUV")
            slotU, snU, wiU = wload(l, ("inU", 0))
            wvu = slotU[:, 0:4096].rearrange("p (k c) -> p k c", k=KC)
            slotV, snV, wiV = wload(l, ("inV", 0))
            wvv = slotV[:, 0:4096].rearrange("p (k c) -> p k c", k=KC)

            goff = [0]
            for sb_ in subs:
                goff.append(goff[-1] + sb_[3] // 128 + (1 if sb_[4] else 0))
            assert goff[-1] <= 9

            def gidx(si, g):
                return goff[si] + g

            def groups(si):
                c0, n, t0_, npr, nsa = subs[si]
                gs = [(g, g * 128, 128, False) for g in range(npr // 128)]
                if nsa:
                    gs.append((npr // 128, npr, nsa, True))
                return gs

            def u_chunk(si, c):
                c0, n, t0_, npr, nsa = subs[si]
                pu, pun = bank("mm")
                for k in range(KC):
                    MM(pu[:, 0:n], wvu[:, k, c * 128:(c + 1) * 128], Hs(si, k), k == 0, k == KC - 1,
                       r=[snU, f"H.{si}.{k}"], w=[pun])
                ACT(HDs(si, 8 + c), pu[:, 0:n], AF.Gelu_apprx_tanh, r=[pun], w=[f"U.{si}.{c}"])

            vdef = [None]

            def vdefer(fn):
                if vdef[0] is not None:
                    vdef[0]()
                vdef[0] = fn

            def v_group(si, g):
                _g, gc0, nt, is_s = groups(si)[g]
                gi = gidx(si, g)
                pb_ = gi % 2
                STATS, MV = STATS2[:, pb_], MV2[:, pb_]
                VE, RS = VEt[:, pb_ * 4:pb_ * 4 + 4], RSt[:, pb_ * 4:pb_ * 4 + 4]
                pvv, pvvn = bank("mm")
                for k in range(KC):
                    MM(pvv[0:nt, :], Hs(si, k, gc0, gc0 + nt), wvv[:, k, :], k == 0, k == KC - 1,
                       r=[snV, f"H.{si}.{k}"], w=[pvvn])
                gv, gvn = scr()
                ACT(gv[0:nt, :], pvv[0:nt, :], AF.Gelu_apprx_tanh, r=[pvvn], w=[gvn])
                for hd in range(4):
                    S.add("dve", lambda e, hd=hd, gv=gv, nt=nt, STATS=STATS: e.bn_stats(out=STATS[0:nt, hd, :], in_=gv[0:nt, hd * 128:(hd + 1) * 128]),
                          r=[gvn], w=[f"STATS.{pb_}.{hd}"])
                for hd in range(4):
                    S.add("dve", lambda e, hd=hd, nt=nt, STATS=STATS, MV=MV: e.bn_aggr(out=MV[0:nt, hd, :], in_=STATS[0:nt, hd, :]),
                          r=[f"STATS.{pb_}.{hd}"], w=[f"MV.{pb_}.{hd}"])
                mvn = [f"MV.{pb_}.{hd}" for hd in range(4)]
                TS(VE[0:nt, :], MV[0:nt, :, 1], EPS, None, ALU.add, None, r=mvn, w=[f"VE.{pb_}"])
                IT, T4 = ITt[:, pb_ * 4:pb_ * 4 + 4], T4t[:, pb_ * 4:pb_ * 4 + 4]
                itn, t4n, rsn, ven = f"IT.{pb_}", f"T4.{pb_}", f"RS.{pb_}", f"VE.{pb_}"
                TS(IT[0:nt, :], VE[0:nt, :].bitcast(mybir.dt.int32), 1, None, ALU.logical_shift_right, None, r=[ven], w=[itn])
                TS(IT[0:nt, :], IT[0:nt, :], -1.0, 1597463007.0, ALU.mult, ALU.add, r=[itn], w=[itn])
                cur, curn = IT[0:nt, :].bitcast(F32), itn
                for _it in range(2):
                    TT_(T4[0:nt, :], cur, cur, ALU.mult, r=[curn], w=[t4n], eng="pool")
                    TT_(T4[0:nt, :], T4[0:nt, :], VE[0:nt, :], ALU.mult, r=[t4n, ven], w=[t4n], eng="pool")
                    TS(T4[0:nt, :], T4[0:nt, :], -0.5, 1.5, ALU.mult, ALU.add, r=[t4n], w=[t4n], eng="pool")
                    TT_(RS[0:nt, :], cur, T4[0:nt, :], ALU.mult, r=[curn, t4n], w=[rsn], eng="pool")
                    cur, curn = RS[0:nt, :], rsn

                def normalize(si=si, g=g, gi=gi, nt=nt, gv=gv, gvn=gvn, is_s=is_s, MV=MV, RS=RS, pb_=pb_):
                    for hd in range(4):
                        hs = slice(hd * 128, (hd + 1) * 128)
                        TS(V[0:nt, gi, hs], gv[0:nt, hs], MV[0:nt, hd, 0:1], RS[0:nt, hd:hd + 1], ALU.subtract, ALU.mult,
                           r=[gvn, f"MV.{pb_}.{hd}", f"RS.{pb_}"], w=[f"V.{gi}"])
                    if is_s:
                        vn, vnn = scr()
                        for hd in range(4):
                            hs = slice(hd * 128, (hd + 1) * 128)
                            TS(vn[0:nt, hs], gv[0:nt, hs], MV[0:nt, hd, 0:1], RS[0:nt, hd:hd + 1], ALU.subtract, ALU.mult,
                               r=[gvn, f"MV.{pb_}.{hd}", f"RS.{pb_}"], w=[vnn])
                        TT_(vn[0:nt, :], vn[0:nt, :], LNG[0:nt, :], ALU.mult, r=[vnn, "CB.0"], w=[vnn])
                        TT_(vn[0:nt, :], vn[0:nt, :], LNB[0:nt, :], ALU.add, r=[vnn, "CB.1"], w=[vnn])
                        DMA("sp", vs[l], vn[0:nt, :], r=[vnn], w=(), chan="c:out")
                vdefer(normalize)

            def spatial(si):
                c0, n, t0_, npr, nsa = subs[si]
                ngp = npr // 128
                for hd in range(4):
                    sp, spn = bank("mm")
                    for (g, gc0, nt, is_s) in groups(si):
                        gi = gidx(si, g)
                        rhs, rr = (WSS[:, l, hd, :], ["WSS"]) if is_s else (WSTB[:, l, hd, :], ["WSTB"])
                        MM(sp[:, gc0:gc0 + nt], V[0:nt, gi, hd * 128:(hd + 1) * 128], rhs, True, True,
                           r=[f"V.{gi}"] + rr, w=[spn])
                    gcol = vec(l, V_BLG + hd)
                    spv = sp[:, 0:npr].rearrange("p (g i) -> p g i", g=ngp)
                    chb = CHt[:, hd * 128:(hd + 1) * 128].rearrange("p (o i) -> p o i", o=1).to_broadcast([128, ngp, 128])
                    STT(spv, spv, gcol, chb, ALU.mult, ALU.add, r=[spn, "VEC", "CH"], w=[spn])
                    if nsa:
                        STT(sp[:, npr:n], sp[:, npr:n], gcol, CHS[:, hd, :], ALU.mult, ALU.add, r=[spn, "VEC", "CHS"], w=[spn])
                    TT_(HDs(si, 2 + hd), HDs(si, 8 + hd), sp[:, 0:n], ALU.mult,
                        r=[f"U.{si}.{hd}", spn], w=[f"MIX.{si}.{2 + hd}"])

            for si in range(nsub):
                ngs = len(groups(si))
                for q in range(max(4, ngs)):
                    if q < ngs:
                        v_group(si, q)
                    if q < 4:
                        u_chunk(si, q)
            vdefer(None)
            wrelease(wiU)
            wrelease(wiV)

            mark(f"t{ti}l{l}.mixC")
            slot, sn, wiC = wload(l, ("inC", 0))
            wv = slot[:, 0:4096].rearrange("p (k c) -> p k c", k=KC)

            def c_proj(si):
                c0, n, t0_, npr, nsa = subs[si]
                for ch in range(2):
                    pval, pvaln = bank("mm")
                    pg, pgn = bank("mm")
                    for (pp, ppn, co) in ((pval, pvaln, ch * 128), (pg, pgn, 256 + ch * 128)):
                        for k in range(KC):
                            MM(pp[:, 0:n], wv[:, k, co:co + 128], Hs(si, k), k == 0, k == KC - 1,
                               r=[sn, f"H.{si}.{k}"], w=[ppn])
                    sg, sgn = scr()
                    ACT(sg[:, 0:n], pg[:, 0:n], AF.Sigmoid, r=[pgn], w=[sgn])
                    TT_(FCX[:, ch, 30 + t0_:30 + t0_ + npr], sg[:, 0:npr], pval[:, 0:npr], ALU.mult, r=[sgn, pvaln], w=[f"FCX.{ch}.{si}"])
                    if last_tile and si == nsub - 1:
                        TT_(CCO[:, l, ch, :], sg[:, npr - 30:npr], pval[:, npr - 30:npr], ALU.mult, r=[sgn, pvaln], w=[f"CCO.{l}"])
                    if nsa:
                        TT_(FCS[:, ch, 30, :], sg[:, npr:n], pval[:, npr:n], ALU.mult, r=[sgn, pvaln], w=[f"FCS.n.{ch}"])
                        TT_(CCSN[:, l, ch, :], sg[:, npr:n], pval[:, npr:n], ALU.mult, r=[sgn, pvaln], w=[f"CCSN.{l}"])

            def conv_c(si):
                c0, n, t0_, npr, nsa = subs[si]
                for ch in range(2):
                    cv, cvn = bank("mm")
                    for k in range(31):
                        MM(cv[:, 0:npr], DIAGC[:, ch, k, :], FCX[:, ch, t0_ + k:t0_ + k + npr], k == 0, k == 30,
                           r=[f"FCX.{ch}.{si}", prev("FCX", ch, si), f"DIAGC.{ch}"], w=[cvn])
                    if nsa:
                        for k in range(31):
                            MM(cv[:, npr:n], DIAGC[:, ch, k, :], FCS[:, ch, k, :], k == 0, k == 30,
                               r=["FCS.h", f"FCS.n.{ch}", f"DIAGC.{ch}"], w=[cvn])
                    ACT(CB[:, ch, 0:n], cv[:, 0:n], AF.Identity, r=[cvn, "VEC"], w=[f"CB.{ch}"], bias=vec(l, V_CCB + ch), scale=1.0)
                    ACT(CBR[:, ch, 0:n], CB[:, ch, 0:n], AF.Copy, r=[f"CB.{ch}"], w=[f"CBR.{ch}"])
                    ACT(CSQ[:, ch, 0:n], CB[:, ch, 0:n], AF.Square, r=[f"CB.{ch}"], w=[f"CSQ.{ch}"])

            def ln_c(si):
                c0, n, t0_, npr, nsa = subs[si]
                mp, mpn = bank("st")
                ep, epn = bank("st")
                for ch in range(2):
                    MM(mp[:, 0:n], ONES256[:], CBR[:, ch, 0:n], ch == 0, ch == 1, r=[f"CBR.{ch}", "ONES256"], w=[mpn])
                for ch in range(2):
                    MM(ep[:, 0:n], ONES256[:], CSQ[:, ch, 0:n], ch == 0, ch == 1, r=[f"CSQ.{ch}", "ONES256"], w=[epn])
                ms, msn = scr()
                ACT(ms[:, 0:n], mp[:, 0:n], AF.Square, r=[mpn], w=[msn])
                TT_(ms[:, 0:n], ep[:, 0:n], ms[:, 0:n], ALU.subtract, r=[epn, msn], w=[msn])
                ACT(ms[:, 0:n], ms[:, 0:n], AF.Ln, r=[msn], w=[msn], bias=EPS, scale=1.0)
                ACT(ep[:, 0:n], ms[:, 0:n], AF.Exp, r=[msn], w=[epn], scale=-0.5)
                for ch in range(2):
                    t1, t1n = scr()
                    TT_(t1[:, 0:n], CB[:, ch, 0:n], mp[:, 0:n], ALU.subtract, r=[f"CB.{ch}", mpn], w=[t1n])
                    TT_(t1[:, 0:n], t1[:, 0:n], ep[:, 0:n], ALU.mult, r=[t1n, epn], w=[t1n])
                    ACT(HDs(si, 6 + ch), t1[:, 0:n], AF.Silu, r=[t1n, "VEC"], w=[f"MIX.{si}.{6 + ch}"],
                        bias=vec(l, V_CLB + ch), scale=vec(l, V_CLG + ch))

            for si in range(nsub):
                c_proj(si)
                spatial(si)
                if si > 1:
                    ln_c(si - 2)
                if si > 0:
                    conv_c(si - 1)
            wrelease(wiC)
            if nsub > 1:
                ln_c(nsub - 2)
            conv_c(nsub - 1)
            ln_c(nsub - 1)

            if not last_tile:
                ACT(FAH[:, l, :, :], FAX[:, :, TT:TT + 2], AF.Copy, r=[f"FAX.0.{nsub - 1}", f"FAX.1.{nsub - 1}"], w=["FAH"])
                ACT(FCH[:, l, :, :], FCX[:, :, TT:TT + 30], AF.Copy, r=[f"FCX.0.{nsub - 1}", f"FCX.1.{nsub - 1}"], w=["FCH"])
            else:
                DMA("sp", cap[l].rearrange("(c p) k -> p c k", p=128), CAO[:, l, :, :], r=[f"CAO.{l}"], w=(), chan="c:out")
                DMA("sp", ccp[l].rearrange("(c p) k -> p c k", p=128), CCO[:, l, :, :], r=[f"CCO.{l}"], w=(), chan="c:out")
            if has_s:
                DMA("sp", casn[l].rearrange("(c p) s -> p c s", p=128), CASN[:, l, :, :], r=[f"CASN.{l}"], w=(), chan="c:out")
                DMA("sp", ccsn[l].rearrange("(c p) s -> p c s", p=128), CCSN[:, l, :, :], r=[f"CCSN.{l}"], w=(), chan="c:out")

            mark(f"t{ti}l{l}.wout")
            S.transfer(Hn, YSBn)
            ys = (YStats(streams[0]), YStats(streams[1]))

            def out_work(hh, si, slot, sn, ys=ys, part=None):
                c0, n, t0_, npr, nsa = subs[si]
                wv = slot[:, 0:4096].rearrange("p (k c) -> p k c", k=KC)
                for ml in (range(4) if part is None else range(2 * part, 2 * part + 2)):
                    m = 4 * hh + ml
                    yp, ypn = bank("mm")
                    for k in range(KC):
                        MM(yp[:, 0:n], wv[:, k, ml * 128:(ml + 1) * 128], HDs(si, k), k == 0, k == KC - 1,
                           r=[sn, f"MIX.{si}.{k}"], w=[ypn])
                    ACT(Ys(si, m), yp[:, 0:n], AF.Copy, r=[ypn], w=[f"YSB.{si}.{m}"])
                    ys[0 if si in streams[0] else 1].add(si, yp[:, 0:n], [ypn], n)
                    tick()

            gm = vec(l, G_MPOST, 8)
            run_phase(l, [("out", 0), ("out", 1)], out_work, head=0, tail=2, ys=ys, gpc=4, parts=2,
                      mk_bnd=lambda sis, y, gm=gm, l=l: Bnd(sis, gm, Ys, "YSB", y, (l, G_F2PRE)))
            S.transfer(MIXERn, HIDn)

            mark(f"t{ti}l{l}.ffn2")
            ffn(l, 2, VECH[:, l * 16 + 8:l * 16 + 16], (l, G_EPRE))

            mark(f"t{ti}l{l}.ple")
            S.transfer(HIDn, YSB2n)
            pslot, psn, wiP = wload(l, ("ewp", 0))
            wp = pslot[:, 0:2048].rearrange("p (k c) -> p k c", k=2)
            ys = (YStats(streams[0]), YStats(streams[1]))

            def ple_work(hh, si, slot, sn, ys=ys, wp=wp, psn=psn, part=None):
                c0, n, t0_, npr, nsa = subs[si]
                wv = slot[:, 0:4096].rearrange("p (k c) -> p k c", k=KC)
                ptn = "PT"
                for ml in (range(4) if part is None else range(2 * part, 2 * part + 2)):
                    m = 4 * hh + ml
                    gp, gpn = bank("mm")
                    pp, ppn = bank("mm")
                    for k in range(KC):
                        MM(gp[:, 0:n], wv[:, k, ml * 128:(ml + 1) * 128], Hs(si, k), k == 0, k == KC - 1,
                           r=[sn, f"H.{si}.{k}"], w=[gpn])
                    for k in range(2):
                        MM(pp[:, 0:n], wp[:, k, m * 128:(m + 1) * 128], PT[:, k, c0:c0 + n], k == 0, k == 1,
                           r=[psn, ptn], w=[ppn])
                    t, tn = scr()
                    ACT(t[:, 0:n], gp[:, 0:n], AF.Sigmoid, r=[gpn], w=[tn])
                    TT_(Ys2(si, m), t[:, 0:n], pp[:, 0:n], ALU.mult, r=[tn, ppn], w=[f"YSB2.{si}.{m}"])
                    ys[0 if si in streams[0] else 1].add(si, Ys2(si, m), [f"YSB2.{si}.{m}"], n)
                    tick()

            nxt = (l + 1, G_F1PRE) if l + 1 < L else None
            ge = vec(l, G_EPOST, 8)
            run_phase(l, [("ewg", 0), ("ewg", 1)], ple_work, head=2, tail=2, ys=ys, gpc=4, parts=2,
                      mk_bnd=lambda sis, y, ge=ge, nxt=nxt: Bnd(sis, ge, Ys2, "YSB2", y, nxt))
            wrelease(wiP)
            S.transfer(YSB2n, HIDn)

        mark(f"t{ti}.end")
        resolve_pending()
        yv = yT.rearrange("(k p) t -> p k t", p=128)
        for (k_, a_, b_, ta_, tb_, sis_) in colmap():
            nms = [x for si_ in sis_ for x in nm("XT", si_, 8)]
            if k_ == "p":
                DMA("sp", yv[:, :, tok0 + ta_:tok0 + tb_], XT[:, :, a_:b_], r=nms, w=(), chan="c:out")
            else:
                DMA("sp", ysT.rearrange("(k p) t -> p k t", p=128), XT[:, :, a_:b_], r=nms, w=(), chan="c:out")

    S.finalize()
    keys = set(S.chan_cnt.keys()) | {"pe", "act", "dve", "pool"}
    sems = {k: es.enter_context(nc.semaphore(k.replace(":", "_"))) for k in sorted(keys)}
    with nc.Block() as block:
        @block.sync
        def _(eng):
            S.emit("sp", eng, sems, final_waits=out_chans)

        @block.gpsimd
        def _(eng):
            S.emit("pool", eng, sems)

        @block.scalar
        def _(eng):
            S.emit("act", eng, sems)

        @block.vector
        def _(eng):
            S.emit("dve", eng, sems)

        @block.tensor
        def _(eng):
            S.emit("pe", eng, sems)
    es.close()
    return nc


def kernel(**inp):
    inp = {k: np.asarray(v) for k, v in inp.items()}
    Wp = pack_weights(inp)
    VECp = pack_vec(inp)
    ROWS = np.ascontiguousarray(np.stack([inp["b_ln_g"].reshape(L, 512), inp["b_ln_b"].reshape(L, 512),
                                          inp["b_bias"].reshape(L, 512)], axis=1)).astype(np.float32)
    WST = np.ascontiguousarray(inp["b_ws"].transpose(0, 3, 1, 2).reshape(L, 128, 512)).astype(np.float32)
    WS00 = np.ascontiguousarray(inp["b_ws"][:, :, 0, 0].reshape(1, L * 4)).astype(np.float32)
    CONSTS = np.concatenate([np.eye(128, dtype=np.float32), np.triu(np.ones((128, 128), np.float32))], axis=1)
    in_maps = []
    for c in range(NCORES):
        ss = slice(NS * c, NS * (c + 1))
        in_maps.append({
            "xT": np.ascontiguousarray(inp["x_prompt"][c].T),
            "xsT": np.ascontiguousarray(inp["x_sample"][ss, 0, :].T),
            "pT": np.ascontiguousarray(inp["p_prompt"][:, c].transpose(0, 2, 1)),
            "psT": np.ascontiguousarray(inp["p_sample"][:, ss, 0, :].transpose(0, 2, 1)),
            "sa": np.ascontiguousarray(inp["state_conv_a"][:, ss]),
            "saT": np.ascontiguousarray(inp["state_conv_a"][:, ss].transpose(0, 3, 2, 1)),
            "sc": np.ascontiguousarray(inp["state_conv_c"][:, ss]),
            "scT": np.ascontiguousarray(inp["state_conv_c"][:, ss].transpose(0, 3, 2, 1)),
            "W": Wp, "VEC": VECp, "ROWS": ROWS, "WST": WST, "WS00": WS00, "CONSTS": CONSTS,
        })
    nc = build_program()
    res = run_bass_kernel_spmd(nc, in_maps, core_ids=list(range(NCORES)))
    R = res.results
    y_prompt = np.stack([R[c]["yT"].T for c in range(NCORES)], axis=0)
    y_sample = np.concatenate([R[c]["ysT"].T for c in range(NCORES)], axis=0)[:, None, :]
    conv_a_prompt = np.stack([R[c]["cap"].transpose(0, 2, 1) for c in range(NCORES)], axis=1)
    conv_c_prompt = np.stack([R[c]["ccp"].transpose(0, 2, 1) for c in range(NCORES)], axis=1)
    conv_a_sample = np.concatenate(
        [np.concatenate([R[c]["cash"], R[c]["casn"].transpose(0, 2, 1)[:, :, None, :]], axis=2) for c in range(NCORES)], axis=1)
    conv_c_sample = np.concatenate(
        [np.concatenate([R[c]["ccsh"], R[c]["ccsn"].transpose(0, 2, 1)[:, :, None, :]], axis=2) for c in range(NCORES)], axis=1)
    v_rows = np.concatenate([R[c]["vs"] for c in range(NCORES)], axis=1)[:, :, None, :]
    f = lambda a: np.ascontiguousarray(a, dtype=np.float32)
    return (f(y_prompt), f(y_sample), f(conv_a_prompt), f(conv_c_prompt), f(conv_a_sample), f(conv_c_sample), f(v_rows))
```

```python
import numpy as np
from contextlib import ExitStack
import concourse.bass as bass
import concourse.mybir as mybir
from concourse.bass_utils import run_bass_kernel_spmd

F32 = mybir.dt.float32
BF16 = mybir.dt.bfloat16
F32R = mybir.dt.float32r
AF = mybir.ActivationFunctionType
ALU = mybir.AluOpType

NCORES = 8
L = 2
D = 1024
KC = 8
DFF = 2816
JC = 22
SEQ = 2048
NS = 16
TT = 1024
NT = TT + NS
EPS = 1e-6
NVL = 146
NSLOT = 4
SLOT = 4096
NSCR = 6
STAT_F32R = False
SAME_ENG_SYNC = True
ONE_STREAM = False
EARLY_PARTB = True
MARKS = []


def wplan():
    plan = []
    for f in (1, 2):
        if f == 2:
            pass
        ups = [((f"f{f}u", jj), 4096) for jj in range(11)]
        downs = [((f"f{f}d", m), 2816) for m in range(8)]
        if f == 1:
            plan += ups + downs
            plan += [(("inA", 0), 3072), (("inA", 1), 3072), (("inU", 0), 4096), (("inV", 0), 4096),
                     (("inC", 0), 4096), (("out", 0), 4096), (("out", 1), 4096)]
        else:
            plan += ups + downs
    plan += [(("ewp", 0), 2048), (("ewg", 0), 4096), (("ewg", 1), 4096)]
    return plan


PLAN = wplan()
WL = sum(x for _, x in PLAN)
WOFF = {}
_o = 0
for _k, _x in PLAN:
    WOFF[_k] = (_o, _x)
    _o += _x
WTOT = L * WL


def _blk(Wm, cols):
    K = Wm.shape[0]
    sub = Wm[:, cols]
    return sub.reshape(K // 128, 128, sub.shape[1]).transpose(1, 0, 2)


def pack_weights(inp):
    out = np.empty((128, WTOT), np.float32)
    for l in range(L):
        for key, X in PLAN:
            off = l * WL + WOFF[key][0]
            kind, i = key
            if kind in ("f1u", "f2u"):
                f = kind[1]
                c = np.arange(256 * i, 256 * i + 256)
                a = np.stack([_blk(inp[f"f{f}_wg"][l], c), _blk(inp[f"f{f}_wu"][l], c)], axis=1)
            elif kind in ("f1d", "f2d"):
                f = kind[1]
                a = _blk(inp[f"f{f}_wd"][l], np.arange(128 * i, 128 * i + 128))
            elif kind == "inA":
                c = np.concatenate([np.arange(b + 128 * i, b + 128 * i + 128) for b in (0, 256, 512)])
                a = _blk(inp["w_in"][l], c)
            elif kind == "inU":
                a = _blk(inp["w_in"][l], np.arange(768, 1280))
            elif kind == "inV":
                a = _blk(inp["w_in"][l], np.arange(1280, 1792))
            elif kind == "inC":
                a = _blk(inp["w_in"][l], np.arange(1792, 2304))
            elif kind == "out":
                a = _blk(inp["w_out"][l], np.arange(512 * i, 512 * i + 512))
            elif kind == "ewp":
                a = _blk(inp["e_wp"][l], np.arange(0, 1024))
            elif kind == "ewg":
                a = _blk(inp["e_wg"][l], np.arange(512 * i, 512 * i + 512))
            out[:, off:off + X] = a.reshape(128, X)
    return out


NORM_NAMES = ["f1_pre", "f1_post", "m_pre", "m_post", "f2_pre", "f2_post", "e_pre", "e_post"]
G_F1PRE, G_F1POST, G_MPRE, G_MPOST, G_F2PRE, G_F2POST, G_EPRE, G_EPOST = [8 * i for i in range(8)]
V_ACW = 64
V_CCW = 70
V_CCB = 132
V_CLG = 134
V_CLB = 136
V_BLG = 138
V_BLB = 142


def pack_vec(inp):
    out = np.empty((128, L * NVL), np.float32)
    for l in range(L):
        o = l * NVL
        for i, nm in enumerate(NORM_NAMES):
            out[:, o + 8 * i:o + 8 * i + 8] = inp[nm][l].reshape(8, 128).T
        out[:, o + V_ACW:o + V_ACW + 6] = inp["a_conv_w"][l].reshape(3, 2, 128).transpose(2, 1, 0).reshape(128, 6)
        out[:, o + V_CCW:o + V_CCW + 62] = inp["c_conv_w"][l].reshape(31, 2, 128).transpose(2, 1, 0).reshape(128, 62)
        out[:, o + V_CCB:o + V_CCB + 2] = inp["c_conv_b"][l].reshape(2, 128).T
        out[:, o + V_CLG:o + V_CLG + 2] = inp["c_ln_g"][l].reshape(2, 128).T
        out[:, o + V_CLB:o + V_CLB + 2] = inp["c_ln_b"][l].reshape(2, 128).T
        out[:, o + V_BLG:o + V_BLG + 4] = inp["b_ln_g"][l].T
        out[:, o + V_BLB:o + V_BLB + 4] = inp["b_ln_b"][l].T
    return out


class Op:
    __slots__ = ("eng", "fn", "deps", "signal", "chan", "val")


class Sched:
    ENGS = ("pe", "act", "dve", "pool", "sp")

    def __init__(self):
        self.ops = {e: [] for e in self.ENGS}
        self.lastw = {}
        self.readers = {}
        self.chan_cnt = {}
        self.chan_last = {}

    def add(self, eng, fn, r=(), w=(), chan=None):
        op = Op()
        op.eng, op.fn, op.chan, op.signal, op.val = eng, fn, chan, False, None
        deps = []
        if chan is not None:
            if chan in self.chan_last:
                deps.append(self.chan_last[chan])
            self.chan_last[chan] = op
        for x in r:
            deps.extend(self.lastw.get(x, ()))
        for x in w:
            deps.extend(self.lastw.get(x, ()))
            deps.extend(self.readers.get(x, ()))
        seen = set()
        keep = []
        for d in deps:
            if id(d) in seen:
                continue
            seen.add(id(d))
            if d.chan is None and chan is None and d.eng == eng and (eng == "pe" or not SAME_ENG_SYNC):
                continue
            keep.append(d)
        op.deps = keep
        if chan is not None:
            c = self.chan_cnt.get(chan, 0) + 16
            self.chan_cnt[chan] = c
            op.val = c
        ws = set(w)
        for x in w:
            self.lastw[x] = [op]
            self.readers[x] = []
        for x in r:
            if x in ws:
                continue
            lst = self.readers.setdefault(x, [])
            if chan is None:
                lst[:] = [o for o in lst if not (o.chan is None and o.eng == eng)]
            lst.append(op)
        self.ops[eng].append(op)
        return op

    def transfer(self, old, new):
        ws, rs = [], []
        sw, sr = set(), set()
        for x in old:
            for o in self.lastw.get(x, ()):
                if id(o) not in sw:
                    sw.add(id(o))
                    ws.append(o)
            for o in self.readers.get(x, ()):
                if id(o) not in sr:
                    sr.add(id(o))
                    rs.append(o)
        def compact(lst):
            last = {}
            out = []
            for o in lst:
                if o.chan is None:
                    last[o.eng] = o
                else:
                    out.append(o)
            return out + list(last.values())
        ws = compact(ws)
        rs = compact(rs)
        for y in new:
            self.lastw[y] = list(ws)
            self.readers[y] = list(rs)

    def finalize(self):
        for lst in self.ops.values():
            for op in lst:
                for d in op.deps:
                    if d.chan is None:
                        d.signal = True
        for lst in self.ops.values():
            c = 0
            for op in lst:
                if op.chan is None:
                    if op.signal:
                        c += 1
                    op.val = c

    def emit(self, name, eng, sems, final_waits=()):
        waited = {}
        for op in self.ops[name]:
            need = {}
            for d in op.deps:
                key = d.chan if d.chan is not None else d.eng
                if d.val > need.get(key, 0):
                    need[key] = d.val
            for key, v in need.items():
                if waited.get(key, 0) < v:
                    eng.wait_ge(sems[key], v)
                    waited[key] = v
            ins = op.fn(eng)
            if op.chan is not None:
                ins.then_inc(sems[op.chan], 16)
            elif op.signal:
                ins.then_inc(sems[op.eng], 1)
        for key in final_waits:
            if self.chan_cnt.get(key, 0) > 0:
                eng.wait_ge(sems[key], self.chan_cnt[key])


def build_program():
    nc = bass.Bass("TRN2", target_bir_lowering=False)
    S = Sched()

    def din(name, shape):
        return nc.dram_tensor(name, list(shape), F32, kind="ExternalInput").ap()

    def dout(name, shape):
        return nc.dram_tensor(name, list(shape), F32, kind="ExternalOutput").ap()

    xT = din("xT", [D, SEQ])
    xsT = din("xsT", [D, NS])
    pT = din("pT", [L, 256, SEQ])
    psT = din("psT", [L, 256, NS])
    sa = din("sa", [L, NS, 2, 256])
    saT = din("saT", [L, 256, 2, NS])
    sc = din("sc", [L, NS, 30, 256])
    scT = din("scT", [L, 256, 30, NS])
    Wd = din("W", [128, WTOT])
    VECd = din("VEC", [128, L * NVL])
    ROWS = din("ROWS", [L, 3, 512])
    WSTd = din("WST", [L, 128, 512])
    WS00 = din("WS00", [1, L * 4])
    CONSTS = din("CONSTS", [128, 256])

    yT = dout("yT", [D, SEQ])
    ysT = dout("ysT", [D, NS])
    cap = dout("cap", [L, 256, 2])
    ccp = dout("ccp", [L, 256, 30])
    cash = dout("cash", [L, NS, 1, 256])
    casn = dout("casn", [L, 256, NS])
    ccsh = dout("ccsh", [L, NS, 29, 256])
    ccsn = dout("ccsn", [L, 256, NS])
    vs = dout("vs", [L, NS, 512])

    es = ExitStack()

    def sb(name, shape, dt):
        return es.enter_context(nc.sbuf_tensor(name, list(shape), dt))

    XTt = sb("XT", [128, KC * NT], F32)
    XT = XTt[:].rearrange("p (k t) -> p k t", k=KC)
    R1t = sb("R1", [128, KC * NT], F32)
    YSB = R1t[:].rearrange("p (k t) -> p k t", k=KC)
    H = R1t[:, 0:KC * NT // 2].bitcast(BF16).rearrange("p (k t) -> p k t", k=KC)
    HIDt = sb("HIDR", [128, JC * NT], BF16)
    Vt = sb("V", [128, 9 * 512], BF16)
    V = Vt[:].rearrange("p (g f) -> p g f", g=9)
    RING = [sb(f"RING{i}", [128, SLOT], BF16) for i in range(NSLOT)]
    SQt = sb("SQ", [128, 1024], F32)
    SQS = [sb(f"SQS{i}", [128, 512], BF16) for i in range(4)]
    VECH = sb("VECH", [128, L * 16], F32)
    SDT = F32R if STAT_F32R else BF16
    if STAT_F32R:
        raise NotImplementedError
    else:
        CBR = SQt[:, 0:512].bitcast(BF16).rearrange("p (k t) -> p k t", k=2)
        CSQ = SQt[:, 512:1024].bitcast(BF16).rearrange("p (k t) -> p k t", k=2)
    SCR = [sb(f"SCR{i}", [128, 512], F32) for i in range(NSCR)]
    CBt = sb("CB", [128, 1024], F32)
    CB = CBt[:].rearrange("p (k t) -> p k t", k=2)
    DIAGAt = sb("DIAGA", [128, 6 * 128], BF16)
    DIAGA = DIAGAt[:].rearrange("p (c k m) -> p c k m", c=2, k=3)
    DIAGCt = sb("DIAGC", [128, 62 * 128], BF16)
    DIAGC = DIAGCt[:].rearrange("p (c k m) -> p c k m", c=2, k=31)
    PTt = sb("PT", [128, 2 * NT], BF16)
    PT = PTt[:].rearrange("p (k t) -> p k t", k=2)
    CST = sb("CST", [128, 256], F32)
    IDENT = CST[:, 0:128]
    MASK = CST[:, 128:256]
    ONESB = sb("ONESB", [128, 128], BF16)
    ONES256 = sb("ONES256", [128, 128], SDT)
    CHt = sb("CH", [128, 512], F32)
    CH = CHt[:].rearrange("p (h i) -> p h i", h=4)
    CHSt = sb("CHS", [128, 64], F32)
    CHS = CHSt[:].rearrange("p (h i) -> p h i", h=4)
    S1 = sb("S1", [128, 4], F32)
    ONES16 = sb("ONES16", [128, 16], F32)
    VEt = sb("VE", [128, 8], F32)
    ITt = sb("IT", [128, 8], mybir.dt.int32)
    T4t = sb("T4", [128, 8], F32)
    VECt = sb("VECT", [128, L * NVL], F32)
    LNG = CBt[:, 0:512]
    LNB = CBt[:, 512:1024]
    WSTBt = sb("WSTB", [128, L * 512], BF16)
    WSTB = WSTBt[:].rearrange("p (l h i) -> p l h i", l=L, h=4)
    WSSt = sb("WSS", [16, L * 64], BF16)
    WSS = WSSt[:].rearrange("p (l h i) -> p l h i", l=L, h=4)
    W00T = sb("W00T", [128, L * 4], F32)
    FAHt = sb("FAH", [128, L * 4], BF16)
    FAH = FAHt[:].rearrange("p (l c k) -> p l c k", l=L, c=2)
    FCHt = sb("FCH", [128, L * 60], BF16)
    FCH = FCHt[:].rearrange("p (l c k) -> p l c k", l=L, c=2)
    FASt = sb("FAS", [128, 2 * 3 * NS], BF16)
    FAS = FASt[:].rearrange("p (c k s) -> p c k s", c=2, k=3)
    FCSt = sb("FCS", [128, 2 * 31 * NS], BF16)
    FCS = FCSt[:].rearrange("p (c k s) -> p c k s", c=2, k=31)
    CAOt = sb("CAO", [128, L * 4], F32)
    CAO = CAOt[:].rearrange("p (l c k) -> p l c k", l=L, c=2)
    CCOt = sb("CCO", [128, L * 60], F32)
    CCO = CCOt[:].rearrange("p (l c k) -> p l c k", l=L, c=2)
    CASNt = sb("CASN", [128, L * 2 * NS], F32)
    CASN = CASNt[:].rearrange("p (l c s) -> p l c s", l=L, c=2)
    CCSNt = sb("CCSN", [128, L * 2 * NS], F32)
    CCSN = CCSNt[:].rearrange("p (l c s) -> p l c s", l=L, c=2)
    STATSt = sb("STATS", [128, 48], F32)
    STATS2 = STATSt[:].rearrange("p (b h s) -> p b h s", b=2, h=4)
    MVt = sb("MV", [128, 16], F32)
    MV2 = MVt[:].rearrange("p (b h s) -> p b h s", b=2, h=4)
    RSt = sb("RS", [128, 8], F32)

    PS = [es.enter_context(nc.psum_tensor(f"PS{i}", [128, 512], F32)) for i in range(8)]

    def mark(label):
        MARKS.append((label, len(S.ops["pe"])))

    def MM(out, lhsT, rhs, start, stop, r, w):
        S.add("pe", lambda e: e.matmul(out, lhsT, rhs, start=start, stop=stop), r=r, w=w)

    def ACT(out, in_, func, r, w, bias=None, scale=None):
        kw = {}
        if bias is not None:
            kw["bias"] = bias
        if scale is not None:
            kw["scale"] = scale
        S.add("act", lambda e: e.activation(out=out, in_=in_, func=func, **kw), r=r, w=w)

    def TT_(out, in0, in1, op, r, w, eng="dve"):
        S.add(eng, lambda e: e.tensor_tensor(out=out, in0=in0, in1=in1, op=op), r=r, w=w)

    def STT(out, in0, scalar, in1, op0, op1, r, w):
        S.add("dve", lambda e: e.scalar_tensor_tensor(out=out, in0=in0, scalar=scalar, in1=in1, op0=op0, op1=op1), r=r, w=w)

    def TS(out, in0, s1, s2, op0, op1, r, w, eng="dve"):
        if s2 is None:
            S.add(eng, lambda e: e.tensor_scalar(out=out, in0=in0, scalar1=s1, scalar2=None, op0=op0), r=r, w=w)
        else:
            S.add(eng, lambda e: e.tensor_scalar(out=out, in0=in0, scalar1=s1, scalar2=s2, op0=op0, op1=op1), r=r, w=w)

    def MEMSET(ap, val, w):
        S.add("dve", lambda e: e.memset(ap, val), r=(), w=w)

    chan_rot = {"c:out": [0, 4], "c:const": [0, 8], "c:lay": [0, 5], "c:xin": [0, 3], "c:pin": [0, 3]}

    def DMA(q, out, in_, r, w, chan):
        if chan in chan_rot:
            st = chan_rot[chan]
            chan = f"{chan}{st[0] % st[1]}"
            st[0] += 1
        S.add(q, lambda e: e.dma_start(out=out, in_=in_), r=r, w=w, chan=chan)

    bank_i = {"mm": 0, "st": 0}
    st_held = set()
    bank_split = {"mm": 4}

    def bank(cls, hold=False):
        nmm = bank_split["mm"]
        if cls == "mm":
            b = bank_i["mm"] % nmm
            bank_i["mm"] += 1
            return PS[b], f"PS{b}"
        for _ in range(8 - nmm):
            b = nmm + bank_i["st"] % (8 - nmm)
            bank_i["st"] += 1
            if b not in st_held:
                if hold:
                    st_held.add(b)
                return PS[b], f"PS{b}"
        raise RuntimeError("no free statistics bank")

    def bank_release(name):
        st_held.discard(int(name[2:]))

    scr_i = [0]

    def scr():
        i = scr_i[0] % NSCR
        scr_i[0] += 1
        return SCR[i], f"SCR{i}"

    sqs_i = [0, 0]

    def sqs(stream=0):
        i = 2 * stream + sqs_i[stream] % 2
        sqs_i[stream] += 1
        return SQS[i], f"SQS{i}"

    WSEQ = [(l, key) for _ti in range(2) for l in range(L) for key, _x in PLAN]
    wst = {"cursor": 0, "next": 0, "released": set()}

    def w_try_issue():
        while wst["next"] < len(WSEQ) and (wst["next"] < NSLOT or (wst["next"] - NSLOT) in wst["released"]):
            i = wst["next"]
            l, key = WSEQ[i]
            off, X = WOFF[key]
            off += l * WL
            sl = i % NSLOT
            DMA("pool", RING[sl][:, 0:X], Wd[:, off:off + X], r=(), w=[f"RING{sl}"], chan=f"c:ring{sl}")
            wst["next"] += 1

    def wload(l, key):
        i = wst["cursor"]
        assert WSEQ[i] == (l, key), (WSEQ[i], l, key)
        assert i < wst["next"]
        wst["cursor"] += 1
        sl = i % NSLOT
        return RING[sl], f"RING{sl}", i

    def wrelease(i):
        wst["released"].add(i)
        w_try_issue()

    def vec(l, col, n=1):
        return VECt[:, l * NVL + col:l * NVL + col + n]

    DMA("sp", CST[:], CONSTS, r=(), w=["CST"], chan="c:const")
    DMA("sp", VECt[:], VECd, r=(), w=["VEC"], chan="c:const")
    DMA("sp", W00T[:], WS00.partition_broadcast(128), r=(), w=["W00T"], chan="c:const")
    w_try_issue()
    MEMSET(ONESB[:], 1.0 / D, ["ONESB"])
    MEMSET(ONES256[:], 1.0 / 256.0, ["ONES256"])
    MEMSET(ONES16[:], 1.0, ["ONES16"])
    for l in range(L):
        TS(VECH[:, l * 16:l * 16 + 8], vec(l, G_F1POST, 8), 0.5, None, ALU.mult, None, r=["VEC"], w=["VECH"])
        TS(VECH[:, l * 16 + 8:l * 16 + 16], vec(l, G_F2POST, 8), 0.5, None, ALU.mult, None, r=["VEC"], w=["VECH"])
        t, tn = scr()
        DMA("sp", t[:], WSTd[l], r=(), w=[tn], chan="c:const")
        for hd in range(4):
            TT_(WSTB[:, l, hd, :], t[:, hd * 128:(hd + 1) * 128], MASK, ALU.mult, r=[tn, "CST"], w=["WSTB"])
            TS(WSS[:, l, hd, :], CST[0:16, 0:16], W00T[0:16, l * 4 + hd:l * 4 + hd + 1], None, ALU.mult, None,
               r=["CST", "W00T"], w=["WSS"])

    TILES = [
        [(0, 384 + NS, 0, 384, NS), (384 + NS, 384, 384, 384, 0), (768 + NS, 256, 768, 256, 0)],
        [(0, 512, 0, 512, 0), (512, 512, 512, 512, 0)],
    ]

    def nm(prefix, si, cnt):
        return [f"{prefix}.{si}.{k}" for k in range(cnt)]

    def allnames(prefix, nsub, cnt):
        out = []
        for si in range(nsub):
            out += nm(prefix, si, cnt)
        return out

    out_chans = [f"c:out{i}" for i in range(4)]

    for ti, subs in enumerate(TILES):
        nsub = len(subs)
        has_s = any(sb_[4] for sb_ in subs)
        n0_, n1_ = subs[0][1], subs[1][1]
        FAX = HIDt[:, 14 * n0_:14 * n0_ + 2 * (TT + 2)].rearrange("p (k t) -> p k t", k=2)
        fcx0 = JC * n0_ + 14 * n1_
        FCX = HIDt[:, fcx0:fcx0 + 2 * (TT + 30)].rearrange("p (k t) -> p k t", k=2)
        assert 2 * (TT + 2) <= 8 * n0_ and 2 * (TT + 30) <= 8 * n1_
        tok0 = ti * TT
        last_tile = ti == len(TILES) - 1
        Hn = allnames("H", nsub, 8)
        YSBn = allnames("YSB", nsub, 8)
        HIDn = allnames("HID", nsub, JC)
        YSB2n = allnames("YSB2", nsub, 8)
        MIXERn = (allnames("MIX", nsub, 8) + allnames("U", nsub, 4) + allnames("AB", nsub, 2)
                  + [f"FAX.{c}.{x}" for c in range(2) for x in ["h"] + list(range(nsub))]
                  + [f"FCX.{c}.{x}" for c in range(2) for x in ["h"] + list(range(nsub))])

        xv = xT.rearrange("(k p) t -> p k t", p=128)
        def colmap():
            pieces = []
            for si, (c0, n, t0_, npr, nsa) in enumerate(subs):
                if pieces and pieces[-1][0] == "p" and pieces[-1][2] == c0 and pieces[-1][4] == t0_:
                    k_, a_, b_, ta_, tb_, sis_ = pieces[-1]
                    pieces[-1] = ("p", a_, c0 + npr, ta_, t0_ + npr, sis_ + [si])
                else:
                    pieces.append(("p", c0, c0 + npr, t0_, t0_ + npr, [si]))
                if nsa:
                    pieces.append(("s", c0 + npr, c0 + n, 0, nsa, [si]))
            return pieces

        for (k_, a_, b_, ta_, tb_, sis_) in colmap():
            nms = [x for si_ in sis_ for x in nm("XT", si_, 8)]
            if k_ == "p":
                DMA("sp", XT[:, :, a_:b_], xv[:, :, tok0 + ta_:tok0 + tb_], r=(), w=nms, chan="c:xin")
            else:
                DMA("sp", XT[:, :, a_:b_], xsT.rearrange("(k p) t -> p k t", p=128), r=(), w=nms, chan="c:xin")

        streams = [list(range(nsub)), []] if ONE_STREAM else [[0], list(range(1, nsub))]
        assert not st_held
        bank_split["mm"] = 5 if nsub == 2 else 4

        def Hs(si, k, a=0, b=None):
            c0, n = subs[si][0], subs[si][1]
            hv = R1t[:, 8 * c0:8 * c0 + 4 * n].bitcast(BF16)
            return hv[:, k * n + a:k * n + (n if b is None else b)]

        def Ys(si, m):
            c0, n = subs[si][0], subs[si][1]
            return R1t[:, 8 * c0 + m * n:8 * c0 + (m + 1) * n]

        def Ys2(si, m):
            c0, n = subs[si][0], subs[si][1]
            yv = HIDt[:, JC * c0:JC * c0 + 16 * n].bitcast(F32)
            return yv[:, m * n:(m + 1) * n]

        def HDs(si, j):
            c0, n = subs[si][0], subs[si][1]
            return HIDt[:, JC * c0 + j * n:JC * c0 + (j + 1) * n]
        pend = {"gen": None, "partB": None}

        def step(gen, k):
            if gen is None:
                return
            for _ in range(k):
                try:
                    next(gen)
                except StopIteration:
                    return

        tick_state = {"gen": None, "per": 0}

        def tick():
            if tick_state["gen"] is not None:
                step(tick_state["gen"], tick_state["per"])

        def gen_steps(sis):
            return 1 + KC * len(sis)

        def resolve_pending():
            if pend["partB"] is not None:
                step(pend["gen"], 10 ** 6)
                pend["partB"]()
                pend["gen"] = pend["partB"] = None

        def rstd_inplace(st, stn, n):
            t, tn = scr()
            ACT(t[:, 0:n], st[:, 0:n], AF.Ln, r=[stn], w=[tn], bias=EPS, scale=1.0)
            ACT(st[:, 0:n], t[:, 0:n], AF.Exp, r=[tn], w=[stn], scale=-0.5)

        def h_apply_sub(l2, gcol2, si, st2, st2n):
            c0, n = subs[si][0], subs[si][1]
            for k in range(KC):
                STT(Hs(si, k), XT[:, k, c0:c0 + n], vec(l2, gcol2 + k), st2[:, 0:n], ALU.mult, ALU.mult,
                    r=[f"XT.{si}.{k}", st2n, "VEC"], w=[f"H.{si}.{k}"])

        def prenorm_first(l2, gcol2):
            for si, (c0, n, t0_, npr, nsa) in enumerate(subs):
                st2, st2n = bank("st")
                for k in range(KC):
                    sq, sqn = sqs()
                    ACT(sq[:, 0:n], XT[:, k, c0:c0 + n], AF.Square, r=[f"XT.{si}.{k}"], w=[sqn])
                    MM(st2[:, 0:n], ONESB[:], sq[:, 0:n], k == 0, k == KC - 1, r=[sqn, "ONESB"], w=[st2n])
                rstd_inplace(st2, st2n, n)
                h_apply_sub(l2, gcol2, si, st2, st2n)

        class YStats:
            def __init__(self, sis):
                self.stream = 0 if (len(sis) and sis[0] == 0) else 1
                self.sts = {}
                self.pending = None
                self.cnt = {si: 0 for si in sis}

            def flush(self):
                if self.pending is not None:
                    si, sq, sqn, n = self.pending
                    st, stn = self.sts[si]
                    MM(st[:, 0:n], ONESB[:], sq[:, 0:n], self.cnt[si] == 0, self.cnt[si] == KC - 1, r=[sqn, "ONESB"], w=[stn])
                    self.cnt[si] += 1
                    self.pending = None

            def add(self, si, src_ap, src_res, n):
                if si not in self.sts:
                    self.sts[si] = bank("st", hold=True)
                sq, sqn = sqs(self.stream)
                ACT(sq[:, 0:n], src_ap, AF.Square, r=src_res, w=[sqn])
                self.flush()
                self.pending = (si, sq, sqn, n)

        class Bnd:
            def __init__(self, sis, gains, Y, yname, ys, nxt):
                self.sis, self.gains, self.Y, self.yname, self.ys, self.nxt = sis, gains, Y, yname, ys, nxt

            def partA(self):
                for si in self.sis:
                    st, stn = self.ys.sts[si]
                    rstd_inplace(st, stn, subs[si][1])
                yield
                LAG = 2
                for si in self.sis:
                    c0, n = subs[si][0], subs[si][1]
                    st, stn = self.ys.sts[si]

                    def square(m, si=si, c0=c0, n=n):
                        ww = [f"H.{si}.{m}"] + ([f"YSB.{si}.{m // 2}"] if self.yname == "YSB" else [])
                        ACT(Hs(si, m), XT[:, m, c0:c0 + n], AF.Square, r=[f"XT.{si}.{m}"], w=ww)

                    for m in range(KC + LAG):
                        if m < KC:
                            t, tn = scr()
                            STT(t[:, 0:n], self.Y(si, m), self.gains[:, m:m + 1], st[:, 0:n], ALU.mult, ALU.mult,
                                r=[f"{self.yname}.{si}.{m}", stn, "VEC", "VECH"], w=[tn])
                            TT_(XT[:, m, c0:c0 + n], XT[:, m, c0:c0 + n], t[:, 0:n], ALU.add,
                                r=[tn, f"XT.{si}.{m}"], w=[f"XT.{si}.{m}"], eng="pool")
                        if self.nxt is not None and m >= LAG:
                            square(m - LAG)
                        if m < KC:
                            yield
                    bank_release(stn)

            def partB(self):
                if self.nxt is None:
                    return
                for si in self.sis:
                    c0, n = subs[si][0], subs[si][1]
                    st2, st2n = bank("st")
                    for m in range(KC):
                        MM(st2[:, 0:n], ONESB[:], Hs(si, m), m == 0, m == KC - 1, r=[f"H.{si}.{m}", "ONESB"], w=[st2n])
                    rstd_inplace(st2, st2n, n)
                    h_apply_sub(self.nxt[0], self.nxt[1], si, st2, st2n)

        def run_phase(l, keys, work, head=0, tail=0, ys=None, mk_bnd=None, gpc=1):
            nk = len(keys)
            if head + tail >= nk:
                blocks = [("HT", list(range(nk)))]
            else:
                blocks = []
                if head:
                    blocks.append(("H", list(range(head))))
                blocks.append(("M", list(range(head, nk - tail))))
                if tail:
                    blocks.append(("T", list(range(nk - tail, nk))))
            for kind, idxs in blocks:
                if kind == "M":
                    for p in idxs:
                        slot, sn, wi = wload(l, keys[p])
                        for stm in streams:
                            for si in stm:
                                work(p, si, slot, sn)
                        wrelease(wi)
                    continue
                is_head = "H" in kind
                is_tail = ("T" in kind) and (mk_bnd is not None)
                loaded = [(p,) + wload(l, keys[p]) for p in idxs]
                if is_head and pend["gen"] is not None:
                    ngrp = max(1, len(loaded) * len(streams[0]) * gpc)
                    tick_state["gen"] = pend["gen"]
                    tick_state["per"] = -(-gen_steps(streams[1]) // ngrp)
                for (p, slot, sn, wi) in loaded:
                    for si in streams[0]:
                        work(p, si, slot, sn)
                tick_state["gen"] = None
                if is_head:
                    resolve_pending()
                genA = None
                bA = mk_bnd(streams[0], ys[0]) if is_tail else None
                if is_tail:
                    ys[0].flush()
                    genA = bA.partA()
                    step(genA, 1)
                    ngrp = max(1, len(loaded) * len(streams[1]) * gpc)
                    tick_state["gen"] = genA
                    tick_state["per"] = -(-gen_steps(streams[0]) // ngrp)
                calls = [(p, si, slot, sn) for (p, slot, sn, wi) in loaded for si in streams[1]]
                partb_done = False
                for ci, (p, si, slot, sn) in enumerate(calls):
                    if is_tail and EARLY_PARTB and len(calls) >= 3 and ci == len(calls) - 1:
                        tick_state["gen"] = None
                        step(genA, 10 ** 6)
                        bA.partB()
                        partb_done = True
                    work(p, si, slot, sn)
                tick_state["gen"] = None
                if is_tail:
                    if not partb_done:
                        step(genA, 10 ** 6)
                        bA.partB()
                    ys[1].flush()
                    bB = mk_bnd(streams[1], ys[1])
                    pend["gen"] = bB.partA()
                    step(pend["gen"], 1)
                    pend["partB"] = bB.partB
                for (p, slot, sn, wi) in loaded:
                    wrelease(wi)

        def ffn(l, f, gains_post, nxt):
            def up_work(jj, si, slot, sn):
                c0, n = subs[si][0], subs[si][1]
                wv = slot[:, 0:4096].rearrange("p (a k c) -> p a k c", a=2, k=KC)
                for jl in range(2):
                    j = 2 * jj + jl
                    gp, gpn = bank("mm")
                    up, upn = bank("mm")
                    for k in range(KC):
                        MM(gp[:, 0:n], wv[:, 0, k, jl * 128:(jl + 1) * 128], Hs(si, k), k == 0, k == KC - 1,
                           r=[sn, f"H.{si}.{k}"], w=[gpn])
                    for k in range(KC):
                        MM(up[:, 0:n], wv[:, 1, k, jl * 128:(jl + 1) * 128], Hs(si, k), k == 0, k == KC - 1,
                           r=[sn, f"H.{si}.{k}"], w=[upn])
                    t, tn = scr()
                    ACT(t[:, 0:n], gp[:, 0:n], AF.Silu, r=[gpn], w=[tn])
                    TT_(HDs(si, j), t[:, 0:n], up[:, 0:n], ALU.mult, r=[tn, upn], w=[f"HID.{si}.{j}"])
                    tick()

            run_phase(l, [(f"f{f}u", jj) for jj in range(11)], up_work, head=3, tail=0, gpc=2)
            mark(f"t{ti}l{l}.f{f}down")
            if f == 1:
                for ch in range(2):
                    for k in range(3):
                        TS(DIAGA[:, ch, k, :], IDENT, vec(l, V_ACW + ch * 3 + k), None, ALU.mult, None,
                           r=["CST", "VEC"], w=[f"DIAGA.{ch}"])
                    for k in range(31):
                        TS(DIAGC[:, ch, k, :], IDENT, vec(l, V_CCW + ch * 31 + k), None, ALU.mult, None,
                           r=["CST", "VEC"], w=[f"DIAGC.{ch}"])
            S.transfer(Hn, YSBn)
            ys = (YStats(streams[0]), YStats(streams[1]))

            def down_work(m, si, slot, sn):
                c0, n = subs[si][0], subs[si][1]
                wv = slot[:, 0:2816].rearrange("p (j c) -> p j c", j=JC)
                yp, ypn = bank("mm")
                for j in range(JC):
                    MM(yp[:, 0:n], wv[:, j, :], HDs(si, j), j == 0, j == JC - 1,
                       r=[sn, f"HID.{si}.{j}"], w=[ypn])
                ACT(Ys(si, m), yp[:, 0:n], AF.Copy, r=[ypn], w=[f"YSB.{si}.{m}"])
                ys[0 if si in streams[0] else 1].add(si, yp[:, 0:n], [ypn], n)
                tick()

            run_phase(l, [(f"f{f}d", m) for m in range(KC)], down_work, head=0, tail=3, ys=ys,
                      mk_bnd=lambda sis, y: Bnd(sis, gains_post, Ys, "YSB", y, nxt))

        prenorm_first(0, G_F1PRE)

        for l in range(L):
            if has_s:
                DMA("sp", CBt[:, 0:960].rearrange("p (c x) -> p c x", c=2),
                    scT[l].rearrange("(c p) k s -> p c (k s)", p=128), r=(), w=["CB.0", "CB.1"], chan="c:lay")
                DMA("sp", CBt[:, 960:1024].rearrange("p (c x) -> p c x", c=2),
                    saT[l].rearrange("(c p) k s -> p c (k s)", p=128), r=(), w=["CB.1"], chan="c:lay")
                S.add("act", lambda e: e.activation(out=FCS[:, :, 0:30, :], in_=CBt[:, 0:960].rearrange("p (c k s) -> p c k s", c=2, k=30), func=AF.Copy),
                      r=["CB.0", "CB.1"], w=["FCS.h"])
                S.add("act", lambda e: e.activation(out=FAS[:, :, 0:2, :], in_=CBt[:, 960:1024].rearrange("p (c k s) -> p c k s", c=2, k=2), func=AF.Copy),
                      r=["CB.1"], w=["FAS.h"])
                DMA("sp", cash[l], sa[l][:, 1:2, :], r=(), w=(), chan="c:out")
                DMA("sp", ccsh[l], sc[l][:, 1:30, :], r=(), w=(), chan="c:out")
                DMA("sp", LNG[0:NS, :], ROWS[l, 0:1, :].partition_broadcast(NS), r=(), w=["CB.0"], chan="c:lay")
                DMA("sp", LNB[0:NS, :], ROWS[l, 1:2, :].partition_broadcast(NS), r=(), w=["CB.1"], chan="c:lay")

            mark(f"t{ti}l{l}.ffn1")
            ffn(l, 1, VECH[:, l * 16:l * 16 + 8], (l, G_MPRE))

            mark(f"t{ti}l{l}.mixA")
            S.transfer(HIDn, MIXERn)
            pv = pT[l].rearrange("(k p) t -> p k t", p=128)
            for (k_, a_, b_, ta_, tb_, sis_) in colmap():
                if k_ == "p":
                    DMA("pool", PT[:, :, a_:b_], pv[:, :, tok0 + ta_:tok0 + tb_], r=(), w=["PT"], chan="c:pin")
                else:
                    DMA("pool", PT[:, :, a_:b_], psT[l].rearrange("(k p) t -> p k t", p=128), r=(), w=["PT"], chan="c:pin")
            if ti == 0:
                MEMSET(FAX[:, :, 0:2], 0.0, ["FAX.0.h", "FAX.1.h"])
                MEMSET(FCX[:, :, 0:30], 0.0, ["FCX.0.h", "FCX.1.h"])
            else:
                ACT(FAX[:, :, 0:2], FAH[:, l, :, :], AF.Copy, r=["FAH"], w=["FAX.0.h", "FAX.1.h"])
                ACT(FCX[:, :, 0:30], FCH[:, l, :, :], AF.Copy, r=["FCH"], w=["FCX.0.h", "FCX.1.h"])

            bbc, bbcn = scr()
            DMA("sp", bbc[:], ROWS[l, 2:3, :].partition_broadcast(128), r=(), w=[bbcn], chan="c:lay")
            prs, prsn = bank("mm")
            MM(prs[:, :], ONESB[:], WSTBt[:, l * 512:(l + 1) * 512], True, True, r=["ONESB", "WSTB"], w=[prsn])
            rsum, rsumn = scr()
            ACT(rsum[:], prs[:], AF.Copy, r=[prsn], w=[rsumn], scale=float(D))
            for hd in range(4):
                hs = slice(hd * 128, (hd + 1) * 128)
                STT(CH[:, hd, :], rsum[:, hs], vec(l, V_BLB + hd), bbc[:, hs], ALU.mult, ALU.add,
                    r=[rsumn, bbcn, "VEC"], w=["CH"])
            if has_s:
                for hd in range(4):
                    STT(S1[:, hd:hd + 1], W00T[:, l * 4 + hd:l * 4 + hd + 1], vec(l, V_BLB + hd), bbc[:, hd * 128:hd * 128 + 1],
                        ALU.mult, ALU.add, r=["W00T", "VEC", bbcn], w=["S1"])
                    TS(CHS[:, hd, :], ONES16[:], S1[:, hd:hd + 1], None, ALU.mult, None, r=["ONES16", "S1"], w=["CHS"])

            def prev(prefix, ch, si):
                return f"{prefix}.{ch}.{'h' if si == 0 else si - 1}"

            def conv_a(ch, si):
                c0, n, t0_, npr, nsa = subs[si]
                cv, cvn = bank("mm")
                for k in range(3):
                    MM(cv[:, 0:npr], DIAGA[:, ch, k, :], FAX[:, ch, t0_ + k:t0_ + k + npr], k == 0, k == 2,
                       r=[f"FAX.{ch}.{si}", prev("FAX", ch, si), f"DIAGA.{ch}"], w=[cvn])
                if nsa:
                    for k in range(3):
                        MM(cv[:, npr:n], DIAGA[:, ch, k, :], FAS[:, ch, k, :], k == 0, k == 2,
                           r=["FAS.h", f"FAS.n.{ch}", f"DIAGA.{ch}"], w=[cvn])
                TT_(HDs(si, ch), HDs(si, 12 + ch), cv[:, 0:n], ALU.mult,
                    r=[f"AB.{si}.{ch}", cvn], w=[f"MIX.{si}.{ch}"])

            deferred = [None]

            def defer(fn):
                if deferred[0] is not None:
                    deferred[0]()
                deferred[0] = fn

            def a_work(ch, si, slot, sn):
                c0, n, t0_, npr, nsa = subs[si]
                wv = slot[:, 0:3072].rearrange("p (k c) -> p k c", k=KC)
                pv_, pvn = bank("mm")
                pc, pcn = bank("mm")
                pb, pbn = bank("mm")
                for (pp, ppn, co) in ((pv_, pvn, 0), (pc, pcn, 128), (pb, pbn, 256)):
                    for k in range(KC):
                        MM(pp[:, 0:n], wv[:, k, co:co + 128], Hs(si, k), k == 0, k == KC - 1,
                           r=[sn, f"H.{si}.{k}"], w=[ppn])
                av, avn = scr()
                ACT(av[:, 0:n], pv_[:, 0:n], AF.Copy, r=[pvn], w=[avn])
                TT_(FAX[:, ch, 2 + t0_:2 + t0_ + npr], av[:, 0:npr], pc[:, 0:npr], ALU.mult, r=[avn, pcn], w=[f"FAX.{ch}.{si}"])
                if last_tile and si == nsub - 1:
                    TT_(CAO[:, l, ch, :], av[:, npr - 2:npr], pc[:, npr - 2:npr], ALU.mult, r=[avn, pcn], w=[f"CAO.{l}"])
                if nsa:
                    TT_(FAS[:, ch, 2, :], av[:, npr:n], pc[:, npr:n], ALU.mult, r=[avn, pcn], w=[f"FAS.n.{ch}"])
                    TT_(CASN[:, l, ch, :], av[:, npr:n], pc[:, npr:n], ALU.mult, r=[avn, pcn], w=[f"CASN.{l}"])
                ACT(HDs(si, 12 + ch), pb[:, 0:n], AF.Copy, r=[pbn], w=[f"AB.{si}.{ch}"])
                defer(lambda ch=ch, si=si: conv_a(ch, si))
                tick()

            run_phase(l, [("inA", 0), ("inA", 1)], a_work, head=2, tail=0)
            defer(None)

            mark(f"t{ti}l{l}.mixUV")
            slotU, snU, wiU = wload(l, ("inU", 0))
            wvu = slotU[:, 0:4096].rearrange("p (k c) -> p k c", k=KC)
            slotV, snV, wiV = wload(l, ("inV", 0))
            wvv = slotV[:, 0:4096].rearrange("p (k c) -> p k c", k=KC)

            goff = [0]
            for sb_ in subs:
                goff.append(goff[-1] + sb_[3] // 128 + (1 if sb_[4] else 0))
            assert goff[-1] <= 9

            def gidx(si, g):
                return goff[si] + g

            def groups(si):
                c0, n, t0_, npr, nsa = subs[si]
                gs = [(g, g * 128, 128, False) for g in range(npr // 128)]
                if nsa:
                    gs.append((npr // 128, npr, nsa, True))
                return gs

            def u_chunk(si, c):
                c0, n, t0_, npr, nsa = subs[si]
                pu, pun = bank("mm")
                for k in range(KC):
                    MM(pu[:, 0:n], wvu[:, k, c * 128:(c + 1) * 128], Hs(si, k), k == 0, k == KC - 1,
                       r=[snU, f"H.{si}.{k}"], w=[pun])
                ACT(HDs(si, 8 + c), pu[:, 0:n], AF.Gelu_apprx_tanh, r=[pun], w=[f"U.{si}.{c}"])

            vdef = [None]

            def vdefer(fn):
                if vdef[0] is not None:
                    vdef[0]()
                vdef[0] = fn

            def v_group(si, g):
                _g, gc0, nt, is_s = groups(si)[g]
                gi = gidx(si, g)
                pb_ = gi % 2
                STATS, MV = STATS2[:, pb_], MV2[:, pb_]
                VE, RS = VEt[:, pb_ * 4:pb_ * 4 + 4], RSt[:, pb_ * 4:pb_ * 4 + 4]
                pvv, pvvn = bank("mm")
                for k in range(KC):
                    MM(pvv[0:nt, :], Hs(si, k, gc0, gc0 + nt), wvv[:, k, :], k == 0, k == KC - 1,
                       r=[snV, f"H.{si}.{k}"], w=[pvvn])
                gv, gvn = scr()
                ACT(gv[0:nt, :], pvv[0:nt, :], AF.Gelu_apprx_tanh, r=[pvvn], w=[gvn])
                for hd in range(4):
                    S.add("dve", lambda e, hd=hd, gv=gv, nt=nt, STATS=STATS: e.bn_stats(out=STATS[0:nt, hd, :], in_=gv[0:nt, hd * 128:(hd + 1) * 128]),
                          r=[gvn], w=[f"STATS.{pb_}.{hd}"])
                for hd in range(4):
                    S.add("dve", lambda e, hd=hd, nt=nt, STATS=STATS, MV=MV: e.bn_aggr(out=MV[0:nt, hd, :], in_=STATS[0:nt, hd, :]),
                          r=[f"STATS.{pb_}.{hd}"], w=[f"MV.{pb_}.{hd}"])
                mvn = [f"MV.{pb_}.{hd}" for hd in range(4)]
                TS(VE[0:nt, :], MV[0:nt, :, 1], EPS, None, ALU.add, None, r=mvn, w=[f"VE.{pb_}"])
                IT, T4 = ITt[:, pb_ * 4:pb_ * 4 + 4], T4t[:, pb_ * 4:pb_ * 4 + 4]
                itn, t4n, rsn, ven = f"IT.{pb_}", f"T4.{pb_}", f"RS.{pb_}", f"VE.{pb_}"
                TS(IT[0:nt, :], VE[0:nt, :].bitcast(mybir.dt.int32), 1, None, ALU.logical_shift_right, None, r=[ven], w=[itn])
                TS(IT[0:nt, :], IT[0:nt, :], -1.0, 1597463007.0, ALU.mult, ALU.add, r=[itn], w=[itn])
                cur, curn = IT[0:nt, :].bitcast(F32), itn
                for _it in range(2):
                    TT_(T4[0:nt, :], cur, cur, ALU.mult, r=[curn], w=[t4n], eng="pool")
                    TT_(T4[0:nt, :], T4[0:nt, :], VE[0:nt, :], ALU.mult, r=[t4n, ven], w=[t4n], eng="pool")
                    TS(T4[0:nt, :], T4[0:nt, :], -0.5, 1.5, ALU.mult, ALU.add, r=[t4n], w=[t4n], eng="pool")
                    TT_(RS[0:nt, :], cur, T4[0:nt, :], ALU.mult, r=[curn, t4n], w=[rsn], eng="pool")
                    cur, curn = RS[0:nt, :], rsn

                def normalize(si=si, g=g, gi=gi, nt=nt, gv=gv, gvn=gvn, is_s=is_s, MV=MV, RS=RS, pb_=pb_):
                    for hd in range(4):
                        hs = slice(hd * 128, (hd + 1) * 128)
                        TS(V[0:nt, gi, hs], gv[0:nt, hs], MV[0:nt, hd, 0:1], RS[0:nt, hd:hd + 1], ALU.subtract, ALU.mult,
                           r=[gvn, f"MV.{pb_}.{hd}", f"RS.{pb_}"], w=[f"V.{gi}"])
                    if is_s:
                        vn, vnn = scr()
                        for hd in range(4):
                            hs = slice(hd * 128, (hd + 1) * 128)
                            TS(vn[0:nt, hs], gv[0:nt, hs], MV[0:nt, hd, 0:1], RS[0:nt, hd:hd + 1], ALU.subtract, ALU.mult,
                               r=[gvn, f"MV.{pb_}.{hd}", f"RS.{pb_}"], w=[vnn])
                        TT_(vn[0:nt, :], vn[0:nt, :], LNG[0:nt, :], ALU.mult, r=[vnn, "CB.0"], w=[vnn])
                        TT_(vn[0:nt, :], vn[0:nt, :], LNB[0:nt, :], ALU.add, r=[vnn, "CB.1"], w=[vnn])
                        DMA("sp", vs[l], vn[0:nt, :], r=[vnn], w=(), chan="c:out")
                vdefer(normalize)

            def spatial(si):
                c0, n, t0_, npr, nsa = subs[si]
                ngp = npr // 128
                for hd in range(4):
                    sp, spn = bank("mm")
                    for (g, gc0, nt, is_s) in groups(si):
                        gi = gidx(si, g)
                        rhs, rr = (WSS[:, l, hd, :], ["WSS"]) if is_s else (WSTB[:, l, hd, :], ["WSTB"])
                        MM(sp[:, gc0:gc0 + nt], V[0:nt, gi, hd * 128:(hd + 1) * 128], rhs, True, True,
                           r=[f"V.{gi}"] + rr, w=[spn])
                    gcol = vec(l, V_BLG + hd)
                    spv = sp[:, 0:npr].rearrange("p (g i) -> p g i", g=ngp)
                    chb = CHt[:, hd * 128:(hd + 1) * 128].rearrange("p (o i) -> p o i", o=1).to_broadcast([128, ngp, 128])
                    STT(spv, spv, gcol, chb, ALU.mult, ALU.add, r=[spn, "VEC", "CH"], w=[spn])
                    if nsa:
                        STT(sp[:, npr:n], sp[:, npr:n], gcol, CHS[:, hd, :], ALU.mult, ALU.add, r=[spn, "VEC", "CHS"], w=[spn])
                    TT_(HDs(si, 2 + hd), HDs(si, 8 + hd), sp[:, 0:n], ALU.mult,
                        r=[f"U.{si}.{hd}", spn], w=[f"MIX.{si}.{2 + hd}"])

            for si in range(nsub):
                ngs = len(groups(si))
                for q in range(max(4, ngs)):
                    if q < ngs:
                        v_group(si, q)
                    if q < 4:
                        u_chunk(si, q)
            vdefer(None)
            wrelease(wiU)
            wrelease(wiV)

            mark(f"t{ti}l{l}.mixC")
            slot, sn, wiC = wload(l, ("inC", 0))
            wv = slot[:, 0:4096].rearrange("p (k c) -> p k c", k=KC)

            def c_proj(si):
                c0, n, t0_, npr, nsa = subs[si]
                for ch in range(2):
                    pval, pvaln = bank("mm")
                    pg, pgn = bank("mm")
                    for (pp, ppn, co) in ((pval, pvaln, ch * 128), (pg, pgn, 256 + ch * 128)):
                        for k in range(KC):
                            MM(pp[:, 0:n], wv[:, k, co:co + 128], Hs(si, k), k == 0, k == KC - 1,
                               r=[sn, f"H.{si}.{k}"], w=[ppn])
                    sg, sgn = scr()
                    ACT(sg[:, 0:n], pg[:, 0:n], AF.Sigmoid, r=[pgn], w=[sgn])
                    TT_(FCX[:, ch, 30 + t0_:30 + t0_ + npr], sg[:, 0:npr], pval[:, 0:npr], ALU.mult, r=[sgn, pvaln], w=[f"FCX.{ch}.{si}"])
                    if last_tile and si == nsub - 1:
                        TT_(CCO[:, l, ch, :], sg[:, npr - 30:npr], pval[:, npr - 30:npr], ALU.mult, r=[sgn, pvaln], w=[f"CCO.{l}"])
                    if nsa:
                        TT_(FCS[:, ch, 30, :], sg[:, npr:n], pval[:, npr:n], ALU.mult, r=[sgn, pvaln], w=[f"FCS.n.{ch}"])
                        TT_(CCSN[:, l, ch, :], sg[:, npr:n], pval[:, npr:n], ALU.mult, r=[sgn, pvaln], w=[f"CCSN.{l}"])

            def conv_c(si):
                c0, n, t0_, npr, nsa = subs[si]
                for ch in range(2):
                    cv, cvn = bank("mm")
                    for k in range(31):
                        MM(cv[:, 0:npr], DIAGC[:, ch, k, :], FCX[:, ch, t0_ + k:t0_ + k + npr], k == 0, k == 30,
                           r=[f"FCX.{ch}.{si}", prev("FCX", ch, si), f"DIAGC.{ch}"], w=[cvn])
                    if nsa:
                        for k in range(31):
                            MM(cv[:, npr:n], DIAGC[:, ch, k, :], FCS[:, ch, k, :], k == 0, k == 30,
                               r=["FCS.h", f"FCS.n.{ch}", f"DIAGC.{ch}"], w=[cvn])
                    ACT(CB[:, ch, 0:n], cv[:, 0:n], AF.Identity, r=[cvn, "VEC"], w=[f"CB.{ch}"], bias=vec(l, V_CCB + ch), scale=1.0)
                    ACT(CBR[:, ch, 0:n], CB[:, ch, 0:n], AF.Copy, r=[f"CB.{ch}"], w=[f"CBR.{ch}"])
                    ACT(CSQ[:, ch, 0:n], CB[:, ch, 0:n], AF.Square, r=[f"CB.{ch}"], w=[f"CSQ.{ch}"])

            def ln_c(si):
                c0, n, t0_, npr, nsa = subs[si]
                mp, mpn = bank("st")
                ep, epn = bank("st")
                for ch in range(2):
                    MM(mp[:, 0:n], ONES256[:], CBR[:, ch, 0:n], ch == 0, ch == 1, r=[f"CBR.{ch}", "ONES256"], w=[mpn])
                for ch in range(2):
                    MM(ep[:, 0:n], ONES256[:], CSQ[:, ch, 0:n], ch == 0, ch == 1, r=[f"CSQ.{ch}", "ONES256"], w=[epn])
                ms, msn = scr()
                ACT(ms[:, 0:n], mp[:, 0:n], AF.Square, r=[mpn], w=[msn])
                TT_(ms[:, 0:n], ep[:, 0:n], ms[:, 0:n], ALU.subtract, r=[epn, msn], w=[msn])
                ACT(ms[:, 0:n], ms[:, 0:n], AF.Ln, r=[msn], w=[msn], bias=EPS, scale=1.0)
                ACT(ep[:, 0:n], ms[:, 0:n], AF.Exp, r=[msn], w=[epn], scale=-0.5)
                for ch in range(2):
                    t1, t1n = scr()
                    TT_(t1[:, 0:n], CB[:, ch, 0:n], mp[:, 0:n], ALU.subtract, r=[f"CB.{ch}", mpn], w=[t1n])
                    TT_(t1[:, 0:n], t1[:, 0:n], ep[:, 0:n], ALU.mult, r=[t1n, epn], w=[t1n])
                    ACT(HDs(si, 6 + ch), t1[:, 0:n], AF.Silu, r=[t1n, "VEC"], w=[f"MIX.{si}.{6 + ch}"],
                        bias=vec(l, V_CLB + ch), scale=vec(l, V_CLG + ch))

            for si in range(nsub):
                c_proj(si)
                spatial(si)
                if si > 1:
                    ln_c(si - 2)
                if si > 0:
                    conv_c(si - 1)
            wrelease(wiC)
            if nsub > 1:
                ln_c(nsub - 2)
            conv_c(nsub - 1)
            ln_c(nsub - 1)

            if not last_tile:
                ACT(FAH[:, l, :, :], FAX[:, :, TT:TT + 2], AF.Copy, r=[f"FAX.0.{nsub - 1}", f"FAX.1.{nsub - 1}"], w=["FAH"])
                ACT(FCH[:, l, :, :], FCX[:, :, TT:TT + 30], AF.Copy, r=[f"FCX.0.{nsub - 1}", f"FCX.1.{nsub - 1}"], w=["FCH"])
            else:
                DMA("sp", cap[l].rearrange("(c p) k -> p c k", p=128), CAO[:, l, :, :], r=[f"CAO.{l}"], w=(), chan="c:out")
                DMA("sp", ccp[l].rearrange("(c p) k -> p c k", p=128), CCO[:, l, :, :], r=[f"CCO.{l}"], w=(), chan="c:out")
            if has_s:
                DMA("sp", casn[l].rearrange("(c p) s -> p c s", p=128), CASN[:, l, :, :], r=[f"CASN.{l}"], w=(), chan="c:out")
                DMA("sp", ccsn[l].rearrange("(c p) s -> p c s", p=128), CCSN[:, l, :, :], r=[f"CCSN.{l}"], w=(), chan="c:out")

            mark(f"t{ti}l{l}.wout")
            S.transfer(Hn, YSBn)
            ys = (YStats(streams[0]), YStats(streams[1]))

            def out_work(hh, si, slot, sn, ys=ys):
                c0, n, t0_, npr, nsa = subs[si]
                wv = slot[:, 0:4096].rearrange("p (k c) -> p k c", k=KC)
                for ml in range(4):
                    m = 4 * hh + ml
                    yp, ypn = bank("mm")
                    for k in range(KC):
                        MM(yp[:, 0:n], wv[:, k, ml * 128:(ml + 1) * 128], HDs(si, k), k == 0, k == KC - 1,
                           r=[sn, f"MIX.{si}.{k}"], w=[ypn])
                    ACT(Ys(si, m), yp[:, 0:n], AF.Copy, r=[ypn], w=[f"YSB.{si}.{m}"])
                    ys[0 if si in streams[0] else 1].add(si, yp[:, 0:n], [ypn], n)
                    tick()

            gm = vec(l, G_MPOST, 8)
            run_phase(l, [("out", 0), ("out", 1)], out_work, head=0, tail=2, ys=ys, gpc=4,
                      mk_bnd=lambda sis, y, gm=gm, l=l: Bnd(sis, gm, Ys, "YSB", y, (l, G_F2PRE)))
            S.transfer(MIXERn, HIDn)

            mark(f"t{ti}l{l}.ffn2")
            ffn(l, 2, VECH[:, l * 16 + 8:l * 16 + 16], (l, G_EPRE))

            mark(f"t{ti}l{l}.ple")
            S.transfer(HIDn, YSB2n)
            pslot, psn, wiP = wload(l, ("ewp", 0))
            wp = pslot[:, 0:2048].rearrange("p (k c) -> p k c", k=2)
            ys = (YStats(streams[0]), YStats(streams[1]))

            def ple_work(hh, si, slot, sn, ys=ys, wp=wp, psn=psn):
                c0, n, t0_, npr, nsa = subs[si]
                wv = slot[:, 0:4096].rearrange("p (k c) -> p k c", k=KC)
                ptn = "PT"
                for ml in range(4):
                    m = 4 * hh + ml
                    gp, gpn = bank("mm")
                    pp, ppn = bank("mm")
                    for k in range(KC):
                        MM(gp[:, 0:n], wv[:, k, ml * 128:(ml + 1) * 128], Hs(si, k), k == 0, k == KC - 1,
                           r=[sn, f"H.{si}.{k}"], w=[gpn])
                    for k in range(2):
                        MM(pp[:, 0:n], wp[:, k, m * 128:(m + 1) * 128], PT[:, k, c0:c0 + n], k == 0, k == 1,
                           r=[psn, ptn], w=[ppn])
                    t, tn = scr()
                    ACT(t[:, 0:n], gp[:, 0:n], AF.Sigmoid, r=[gpn], w=[tn])
                    TT_(Ys2(si, m), t[:, 0:n], pp[:, 0:n], ALU.mult, r=[tn, ppn], w=[f"YSB2.{si}.{m}"])
                    ys[0 if si in streams[0] else 1].add(si, Ys2(si, m), [f"YSB2.{si}.{m}"], n)
                    tick()

            nxt = (l + 1, G_F1PRE) if l + 1 < L else None
            ge = vec(l, G_EPOST, 8)
            run_phase(l, [("ewg", 0), ("ewg", 1)], ple_work, head=2, tail=2, ys=ys, gpc=4,
                      mk_bnd=lambda sis, y, ge=ge, nxt=nxt: Bnd(sis, ge, Ys2, "YSB2", y, nxt))
            wrelease(wiP)
            S.transfer(YSB2n, HIDn)

        mark(f"t{ti}.end")
        resolve_pending()
        yv = yT.rearrange("(k p) t -> p k t", p=128)
        for (k_, a_, b_, ta_, tb_, sis_) in colmap():
            nms = [x for si_ in sis_ for x in nm("XT", si_, 8)]
            if k_ == "p":
                DMA("sp", yv[:, :, tok0 + ta_:tok0 + tb_], XT[:, :, a_:b_], r=nms, w=(), chan="c:out")
            else:
                DMA("sp", ysT.rearrange("(k p) t -> p k t", p=128), XT[:, :, a_:b_], r=nms, w=(), chan="c:out")

    S.finalize()
    keys = set(S.chan_cnt.keys()) | {"pe", "act", "dve", "pool"}
    sems = {k: es.enter_context(nc.semaphore(k.replace(":", "_"))) for k in sorted(keys)}
    with nc.Block() as block:
        @block.sync
        def _(eng):
            S.emit("sp", eng, sems, final_waits=out_chans)

        @block.gpsimd
        def _(eng):
            S.emit("pool", eng, sems)

        @block.scalar
        def _(eng):
            S.emit("act", eng, sems)

        @block.vector
        def _(eng):
            S.emit("dve", eng, sems)

        @block.tensor
        def _(eng):
            S.emit("pe", eng, sems)
    es.close()
    return nc


def kernel(**inp):
    inp = {k: np.asarray(v) for k, v in inp.items()}
    Wp = pack_weights(inp)
    VECp = pack_vec(inp)
    ROWS = np.ascontiguousarray(np.stack([inp["b_ln_g"].reshape(L, 512), inp["b_ln_b"].reshape(L, 512),
                                          inp["b_bias"].reshape(L, 512)], axis=1)).astype(np.float32)
    WST = np.ascontiguousarray(inp["b_ws"].transpose(0, 3, 1, 2).reshape(L, 128, 512)).astype(np.float32)
    WS00 = np.ascontiguousarray(inp["b_ws"][:, :, 0, 0].reshape(1, L * 4)).astype(np.float32)
    CONSTS = np.concatenate([np.eye(128, dtype=np.float32), np.triu(np.ones((128, 128), np.float32))], axis=1)
    in_maps = []
    for c in range(NCORES):
        ss = slice(NS * c, NS * (c + 1))
        in_maps.append({
            "xT": np.ascontiguousarray(inp["x_prompt"][c].T),
            "xsT": np.ascontiguousarray(inp["x_sample"][ss, 0, :].T),
            "pT": np.ascontiguousarray(inp["p_prompt"][:, c].transpose(0, 2, 1)),
            "psT": np.ascontiguousarray(inp["p_sample"][:, ss, 0, :].transpose(0, 2, 1)),
            "sa": np.ascontiguousarray(inp["state_conv_a"][:, ss]),
            "saT": np.ascontiguousarray(inp["state_conv_a"][:, ss].transpose(0, 3, 2, 1)),
            "sc": np.ascontiguousarray(inp["state_conv_c"][:, ss]),
            "scT": np.ascontiguousarray(inp["state_conv_c"][:, ss].transpose(0, 3, 2, 1)),
            "W": Wp, "VEC": VECp, "ROWS": ROWS, "WST": WST, "WS00": WS00, "CONSTS": CONSTS,
        })
    nc = build_program()
    res = run_bass_kernel_spmd(nc, in_maps, core_ids=list(range(NCORES)))
    R = res.results
    y_prompt = np.stack([R[c]["yT"].T for c in range(NCORES)], axis=0)
    y_sample = np.concatenate([R[c]["ysT"].T for c in range(NCORES)], axis=0)[:, None, :]
    conv_a_prompt = np.stack([R[c]["cap"].transpose(0, 2, 1) for c in range(NCORES)], axis=1)
    conv_c_prompt = np.stack([R[c]["ccp"].transpose(0, 2, 1) for c in range(NCORES)], axis=1)
    conv_a_sample = np.concatenate(
        [np.concatenate([R[c]["cash"], R[c]["casn"].transpose(0, 2, 1)[:, :, None, :]], axis=2) for c in range(NCORES)], axis=1)
    conv_c_sample = np.concatenate(
        [np.concatenate([R[c]["ccsh"], R[c]["ccsn"].transpose(0, 2, 1)[:, :, None, :]], axis=2) for c in range(NCORES)], axis=1)
    v_rows = np.concatenate([R[c]["vs"] for c in range(NCORES)], axis=1)[:, :, None, :]
    f = lambda a: np.ascontiguousarray(a, dtype=np.float32)
    return (f(y_prompt), f(y_sample), f(conv_a_prompt), f(conv_c_prompt), f(conv_a_sample), f(conv_c_sample), f(v_rows))
```

```python
import numpy as np
from contextlib import ExitStack
import concourse.bass as bass
import concourse.mybir as mybir
from concourse.bass_utils import run_bass_kernel_spmd

F32 = mybir.dt.float32
BF16 = mybir.dt.bfloat16
F32R = mybir.dt.float32r
AF = mybir.ActivationFunctionType
ALU = mybir.AluOpType

NCORES = 8
L = 2
D = 1024
KC = 8
DFF = 2816
JC = 22
SEQ = 2048
NS = 16
TT = 1024
NT = TT + NS
EPS = 1e-6
NVL = 146
NSLOT = 4
SLOT = 4096
NSCR = 6
STAT_F32R = False
SAME_ENG_SYNC = True
ONE_STREAM = False
EARLY_PARTB = True
MARKS = []


def wplan():
    plan = []
    for f in (1, 2):
        if f == 2:
            pass
        ups = [((f"f{f}u", jj), 4096) for jj in range(11)]
        downs = [((f"f{f}d", m), 2816) for m in range(8)]
        if f == 1:
            plan += ups + downs
            plan += [(("inA", 0), 3072), (("inA", 1), 3072), (("inU", 0), 4096), (("inV", 0), 4096),
                     (("inC", 0), 4096), (("out", 0), 4096), (("out", 1), 4096)]
        else:
            plan += ups + downs
    plan += [(("ewp", 0), 2048), (("ewg", 0), 4096), (("ewg", 1), 4096)]
    return plan


PLAN = wplan()
WL = sum(x for _, x in PLAN)
WOFF = {}
_o = 0
for _k, _x in PLAN:
    WOFF[_k] = (_o, _x)
    _o += _x
WTOT = L * WL


def _blk(Wm, cols):
    K = Wm.shape[0]
    sub = Wm[:, cols]
    return sub.reshape(K // 128, 128, sub.shape[1]).transpose(1, 0, 2)


def pack_weights(inp):
    out = np.empty((128, WTOT), np.float32)
    for l in range(L):
        for key, X in PLAN:
            off = l * WL + WOFF[key][0]
            kind, i = key
            if kind in ("f1u", "f2u"):
                f = kind[1]
                c = np.arange(256 * i, 256 * i + 256)
                a = np.stack([_blk(inp[f"f{f}_wg"][l], c), _blk(inp[f"f{f}_wu"][l], c)], axis=1)
            elif kind in ("f1d", "f2d"):
                f = kind[1]
                a = _blk(inp[f"f{f}_wd"][l], np.arange(128 * i, 128 * i + 128))
            elif kind == "inA":
                c = np.concatenate([np.arange(b + 128 * i, b + 128 * i + 128) for b in (0, 256, 512)])
                a = _blk(inp["w_in"][l], c)
            elif kind == "inU":
                a = _blk(inp["w_in"][l], np.arange(768, 1280))
            elif kind == "inV":
                a = _blk(inp["w_in"][l], np.arange(1280, 1792))
            elif kind == "inC":
                a = _blk(inp["w_in"][l], np.arange(1792, 2304))
            elif kind == "out":
                a = _blk(inp["w_out"][l], np.arange(512 * i, 512 * i + 512))
            elif kind == "ewp":
                a = _blk(inp["e_wp"][l], np.arange(0, 1024))
            elif kind == "ewg":
                a = _blk(inp["e_wg"][l], np.arange(512 * i, 512 * i + 512))
            out[:, off:off + X] = a.reshape(128, X)
    return out


NORM_NAMES = ["f1_pre", "f1_post", "m_pre", "m_post", "f2_pre", "f2_post", "e_pre", "e_post"]
G_F1PRE, G_F1POST, G_MPRE, G_MPOST, G_F2PRE, G_F2POST, G_EPRE, G_EPOST = [8 * i for i in range(8)]
V_ACW = 64
V_CCW = 70
V_CCB = 132
V_CLG = 134
V_CLB = 136
V_BLG = 138
V_BLB = 142


def pack_vec(inp):
    out = np.empty((128, L * NVL), np.float32)
    for l in range(L):
        o = l * NVL
        for i, nm in enumerate(NORM_NAMES):
            out[:, o + 8 * i:o + 8 * i + 8] = inp[nm][l].reshape(8, 128).T
        out[:, o + V_ACW:o + V_ACW + 6] = inp["a_conv_w"][l].reshape(3, 2, 128).transpose(2, 1, 0).reshape(128, 6)
        out[:, o + V_CCW:o + V_CCW + 62] = inp["c_conv_w"][l].reshape(31, 2, 128).transpose(2, 1, 0).reshape(128, 62)
        out[:, o + V_CCB:o + V_CCB + 2] = inp["c_conv_b"][l].reshape(2, 128).T
        out[:, o + V_CLG:o + V_CLG + 2] = inp["c_ln_g"][l].reshape(2, 128).T
        out[:, o + V_CLB:o + V_CLB + 2] = inp["c_ln_b"][l].reshape(2, 128).T
        out[:, o + V_BLG:o + V_BLG + 4] = inp["b_ln_g"][l].T
        out[:, o + V_BLB:o + V_BLB + 4] = inp["b_ln_b"][l].T
    return out


class Op:
    __slots__ = ("eng", "fn", "deps", "signal", "chan", "val")


class Sched:
    ENGS = ("pe", "act", "dve", "pool", "sp")

    def __init__(self):
        self.ops = {e: [] for e in self.ENGS}
        self.lastw = {}
        self.readers = {}
        self.chan_cnt = {}
        self.chan_last = {}

    def add(self, eng, fn, r=(), w=(), chan=None):
        op = Op()
        op.eng, op.fn, op.chan, op.signal, op.val = eng, fn, chan, False, None
        deps = []
        if chan is not None:
            if chan in self.chan_last:
                deps.append(self.chan_last[chan])
            self.chan_last[chan] = op
        for x in r:
            deps.extend(self.lastw.get(x, ()))
        for x in w:
            deps.extend(self.lastw.get(x, ()))
            deps.extend(self.readers.get(x, ()))
        seen = set()
        keep = []
        for d in deps:
            if id(d) in seen:
                continue
            seen.add(id(d))
            if d.chan is None and chan is None and d.eng == eng and (eng == "pe" or not SAME_ENG_SYNC):
                continue
            keep.append(d)
        op.deps = keep
        if chan is not None:
            c = self.chan_cnt.get(chan, 0) + 16
            self.chan_cnt[chan] = c
            op.val = c
        ws = set(w)
        for x in w:
            self.lastw[x] = [op]
            self.readers[x] = []
        for x in r:
            if x in ws:
                continue
            lst = self.readers.setdefault(x, [])
            if chan is None:
                lst[:] = [o for o in lst if not (o.chan is None and o.eng == eng)]
            lst.append(op)
        self.ops[eng].append(op)
        return op

    def transfer(self, old, new):
        ws, rs = [], []
        sw, sr = set(), set()
        for x in old:
            for o in self.lastw.get(x, ()):
                if id(o) not in sw:
                    sw.add(id(o))
                    ws.append(o)
            for o in self.readers.get(x, ()):
                if id(o) not in sr:
                    sr.add(id(o))
                    rs.append(o)
        def compact(lst):
            last = {}
            out = []
            for o in lst:
                if o.chan is None:
                    last[o.eng] = o
                else:
                    out.append(o)
            return out + list(last.values())
        ws = compact(ws)
        rs = compact(rs)
        for y in new:
            self.lastw[y] = list(ws)
            self.readers[y] = list(rs)

    def finalize(self):
        for lst in self.ops.values():
            for op in lst:
                for d in op.deps:
                    if d.chan is None:
                        d.signal = True
        for lst in self.ops.values():
            c = 0
            for op in lst:
                if op.chan is None:
                    if op.signal:
                        c += 1
                    op.val = c

    def emit(self, name, eng, sems, final_waits=()):
        waited = {}
        for op in self.ops[name]:
            need = {}
            for d in op.deps:
                key = d.chan if d.chan is not None else d.eng
                if d.val > need.get(key, 0):
                    need[key] = d.val
            for key, v in need.items():
                if waited.get(key, 0) < v:
                    eng.wait_ge(sems[key], v)
                    waited[key] = v
            ins = op.fn(eng)
            if op.chan is not None:
                ins.then_inc(sems[op.chan], 16)
            elif op.signal:
                ins.then_inc(sems[op.eng], 1)
        for key in final_waits:
            if self.chan_cnt.get(key, 0) > 0:
                eng.wait_ge(sems[key], self.chan_cnt[key])


def build_program():
    nc = bass.Bass("TRN2", target_bir_lowering=False)
    S = Sched()

    def din(name, shape):
        return nc.dram_tensor(name, list(shape), F32, kind="ExternalInput").ap()

    def dout(name, shape):
        return nc.dram_tensor(name, list(shape), F32, kind="ExternalOutput").ap()

    xT = din("xT", [D, SEQ])
    xsT = din("xsT", [D, NS])
    pT = din("pT", [L, 256, SEQ])
    psT = din("psT", [L, 256, NS])
    sa = din("sa", [L, NS, 2, 256])
    saT = din("saT", [L, 256, 2, NS])
    sc = din("sc", [L, NS, 30, 256])
    scT = din("scT", [L, 256, 30, NS])
    Wd = din("W", [128, WTOT])
    VECd = din("VEC", [128, L * NVL])
    ROWS = din("ROWS", [L, 3, 512])
    WSTd = din("WST", [L, 128, 512])
    WS00 = din("WS00", [1, L * 4])
    CONSTS = din("CONSTS", [128, 256])

    yT = dout("yT", [D, SEQ])
    ysT = dout("ysT", [D, NS])
    cap = dout("cap", [L, 256, 2])
    ccp = dout("ccp", [L, 256, 30])
    cash = dout("cash", [L, NS, 1, 256])
    casn = dout("casn", [L, 256, NS])
    ccsh = dout("ccsh", [L, NS, 29, 256])
    ccsn = dout("ccsn", [L, 256, NS])
    vs = dout("vs", [L, NS, 512])

    es = ExitStack()

    def sb(name, shape, dt):
        return es.enter_context(nc.sbuf_tensor(name, list(shape), dt))

    XTt = sb("XT", [128, KC * NT], F32)
    XT = XTt[:].rearrange("p (k t) -> p k t", k=KC)
    R1t = sb("R1", [128, KC * NT], F32)
    YSB = R1t[:].rearrange("p (k t) -> p k t", k=KC)
    H = R1t[:, 0:KC * NT // 2].bitcast(BF16).rearrange("p (k t) -> p k t", k=KC)
    HIDt = sb("HIDR", [128, JC * NT], BF16)
    Vt = sb("V", [128, 9 * 512], BF16)
    V = Vt[:].rearrange("p (g f) -> p g f", g=9)
    RING = [sb(f"RING{i}", [128, SLOT], BF16) for i in range(NSLOT)]
    SQt = sb("SQ", [128, 1024], F32)
    SQS = [sb(f"SQS{i}", [128, 512], BF16) for i in range(4)]
    VECH = sb("VECH", [128, L * 16], F32)
    SDT = F32R if STAT_F32R else BF16
    if STAT_F32R:
        raise NotImplementedError
    else:
        CBR = SQt[:, 0:512].bitcast(BF16).rearrange("p (k t) -> p k t", k=2)
        CSQ = SQt[:, 512:1024].bitcast(BF16).rearrange("p (k t) -> p k t", k=2)
    SCR = [sb(f"SCR{i}", [128, 512], F32) for i in range(NSCR)]
    CBt = sb("CB", [128, 1024], F32)
    CB = CBt[:].rearrange("p (k t) -> p k t", k=2)
    DIAGAt = sb("DIAGA", [128, 6 * 128], BF16)
    DIAGA = DIAGAt[:].rearrange("p (c k m) -> p c k m", c=2, k=3)
    DIAGCt = sb("DIAGC", [128, 62 * 128], BF16)
    DIAGC = DIAGCt[:].rearrange("p (c k m) -> p c k m", c=2, k=31)
    PTt = sb("PT", [128, 2 * NT], BF16)
    PT = PTt[:].rearrange("p (k t) -> p k t", k=2)
    CST = sb("CST", [128, 256], F32)
    IDENT = CST[:, 0:128]
    MASK = CST[:, 128:256]
    ONESB = sb("ONESB", [128, 128], BF16)
    ONES256 = sb("ONES256", [128, 128], SDT)
    CHt = sb("CH", [128, 512], F32)
    CH = CHt[:].rearrange("p (h i) -> p h i", h=4)
    CHSt = sb("CHS", [128, 64], F32)
    CHS = CHSt[:].rearrange("p (h i) -> p h i", h=4)
    S1 = sb("S1", [128, 4], F32)
    ONES16 = sb("ONES16", [128, 16], F32)
    VEt = sb("VE", [128, 8], F32)
    ITt = sb("IT", [128, 8], mybir.dt.int32)
    T4t = sb("T4", [128, 8], F32)
    VECt = sb("VECT", [128, L * NVL], F32)
    LNG = CBt[:, 0:512]
    LNB = CBt[:, 512:1024]
    WSTBt = sb("WSTB", [128, L * 512], BF16)
    WSTB = WSTBt[:].rearrange("p (l h i) -> p l h i", l=L, h=4)
    WSSt = sb("WSS", [16, L * 64], BF16)
    WSS = WSSt[:].rearrange("p (l h i) -> p l h i", l=L, h=4)
    W00T = sb("W00T", [128, L * 4], F32)
    FAHt = sb("FAH", [128, L * 4], BF16)
    FAH = FAHt[:].rearrange("p (l c k) -> p l c k", l=L, c=2)
    FCHt = sb("FCH", [128, L * 60], BF16)
    FCH = FCHt[:].rearrange("p (l c k) -> p l c k", l=L, c=2)
    FASt = sb("FAS", [128, 2 * 3 * NS], BF16)
    FAS = FASt[:].rearrange("p (c k s) -> p c k s", c=2, k=3)
    FCSt = sb("FCS", [128, 2 * 31 * NS], BF16)
    FCS = FCSt[:].rearrange("p (c k s) -> p c k s", c=2, k=31)
    CAOt = sb("CAO", [128, L * 4], F32)
    CAO = CAOt[:].rearrange("p (l c k) -> p l c k", l=L, c=2)
    CCOt = sb("CCO", [128, L * 60], F32)
    CCO = CCOt[:].rearrange("p (l c k) -> p l c k", l=L, c=2)
    CASNt = sb("CASN", [128, L * 2 * NS], F32)
    CASN = CASNt[:].rearrange("p (l c s) -> p l c s", l=L, c=2)
    CCSNt = sb("CCSN", [128, L * 2 * NS], F32)
    CCSN = CCSNt[:].rearrange("p (l c s) -> p l c s", l=L, c=2)
    STATSt = sb("STATS", [128, 48], F32)
    STATS2 = STATSt[:].rearrange("p (b h s) -> p b h s", b=2, h=4)
    MVt = sb("MV", [128, 16], F32)
    MV2 = MVt[:].rearrange("p (b h s) -> p b h s", b=2, h=4)
    RSt = sb("RS", [128, 8], F32)

    PS = [es.enter_context(nc.psum_tensor(f"PS{i}", [128, 512], F32)) for i in range(8)]

    def mark(label):
        MARKS.append((label, len(S.ops["pe"])))

    def MM(out, lhsT, rhs, start, stop, r, w):
        S.add("pe", lambda e: e.matmul(out, lhsT, rhs, start=start, stop=stop), r=r, w=w)

    def ACT(out, in_, func, r, w, bias=None, scale=None):
        kw = {}
        if bias is not None:
            kw["bias"] = bias
        if scale is not None:
            kw["scale"] = scale
        S.add("act", lambda e: e.activation(out=out, in_=in_, func=func, **kw), r=r, w=w)

    def TT_(out, in0, in1, op, r, w, eng="dve"):
        S.add(eng, lambda e: e.tensor_tensor(out=out, in0=in0, in1=in1, op=op), r=r, w=w)

    def STT(out, in0, scalar, in1, op0, op1, r, w):
        S.add("dve", lambda e: e.scalar_tensor_tensor(out=out, in0=in0, scalar=scalar, in1=in1, op0=op0, op1=op1), r=r, w=w)

    def TS(out, in0, s1, s2, op0, op1, r, w, eng="dve"):
        if s2 is None:
            S.add(eng, lambda e: e.tensor_scalar(out=out, in0=in0, scalar1=s1, scalar2=None, op0=op0), r=r, w=w)
        else:
            S.add(eng, lambda e: e.tensor_scalar(out=out, in0=in0, scalar1=s1, scalar2=s2, op0=op0, op1=op1), r=r, w=w)

    def MEMSET(ap, val, w):
        S.add("dve", lambda e: e.memset(ap, val), r=(), w=w)

    chan_rot = {"c:out": [0, 4], "c:const": [0, 8], "c:lay": [0, 5], "c:xin": [0, 3], "c:pin": [0, 3]}

    def DMA(q, out, in_, r, w, chan):
        if chan in chan_rot:
            st = chan_rot[chan]
            chan = f"{chan}{st[0] % st[1]}"
            st[0] += 1
        S.add(q, lambda e: e.dma_start(out=out, in_=in_), r=r, w=w, chan=chan)

    bank_i = {"mm": 0, "st": 0}
    st_held = set()

    def bank(cls, hold=False):
        if cls == "mm":
            b = bank_i["mm"] % 4
            bank_i["mm"] += 1
            return PS[b], f"PS{b}"
        for _ in range(4):
            b = 4 + bank_i["st"] % 4
            bank_i["st"] += 1
            if b not in st_held:
                if hold:
                    st_held.add(b)
                return PS[b], f"PS{b}"
        raise RuntimeError("no free statistics bank")

    def bank_release(name):
        st_held.discard(int(name[2:]))

    scr_i = [0]

    def scr():
        i = scr_i[0] % NSCR
        scr_i[0] += 1
        return SCR[i], f"SCR{i}"

    sqs_i = [0, 0]

    def sqs(stream=0):
        i = 2 * stream + sqs_i[stream] % 2
        sqs_i[stream] += 1
        return SQS[i], f"SQS{i}"

    WSEQ = [(l, key) for _ti in range(2) for l in range(L) for key, _x in PLAN]
    wst = {"cursor": 0, "next": 0, "released": set()}

    def w_try_issue():
        while wst["next"] < len(WSEQ) and (wst["next"] < NSLOT or (wst["next"] - NSLOT) in wst["released"]):
            i = wst["next"]
            l, key = WSEQ[i]
            off, X = WOFF[key]
            off += l * WL
            sl = i % NSLOT
            DMA("pool", RING[sl][:, 0:X], Wd[:, off:off + X], r=(), w=[f"RING{sl}"], chan=f"c:ring{sl}")
            wst["next"] += 1

    def wload(l, key):
        i = wst["cursor"]
        assert WSEQ[i] == (l, key), (WSEQ[i], l, key)
        assert i < wst["next"]
        wst["cursor"] += 1
        sl = i % NSLOT
        return RING[sl], f"RING{sl}", i

    def wrelease(i):
        wst["released"].add(i)
        w_try_issue()

    def vec(l, col, n=1):
        return VECt[:, l * NVL + col:l * NVL + col + n]

    DMA("sp", CST[:], CONSTS, r=(), w=["CST"], chan="c:const")
    DMA("sp", VECt[:], VECd, r=(), w=["VEC"], chan="c:const")
    DMA("sp", W00T[:], WS00.partition_broadcast(128), r=(), w=["W00T"], chan="c:const")
    w_try_issue()
    MEMSET(ONESB[:], 1.0 / D, ["ONESB"])
    MEMSET(ONES256[:], 1.0 / 256.0, ["ONES256"])
    MEMSET(ONES16[:], 1.0, ["ONES16"])
    for l in range(L):
        TS(VECH[:, l * 16:l * 16 + 8], vec(l, G_F1POST, 8), 0.5, None, ALU.mult, None, r=["VEC"], w=["VECH"])
        TS(VECH[:, l * 16 + 8:l * 16 + 16], vec(l, G_F2POST, 8), 0.5, None, ALU.mult, None, r=["VEC"], w=["VECH"])
        t, tn = scr()
        DMA("sp", t[:], WSTd[l], r=(), w=[tn], chan="c:const")
        for hd in range(4):
            TT_(WSTB[:, l, hd, :], t[:, hd * 128:(hd + 1) * 128], MASK, ALU.mult, r=[tn, "CST"], w=["WSTB"])
            TS(WSS[:, l, hd, :], CST[0:16, 0:16], W00T[0:16, l * 4 + hd:l * 4 + hd + 1], None, ALU.mult, None,
               r=["CST", "W00T"], w=["WSS"])

    TILES = [
        [(0, 384 + NS, 0, 384, NS), (384 + NS, 384, 384, 384, 0), (768 + NS, 256, 768, 256, 0)],
        [(0, 512, 0, 512, 0), (512, 512, 512, 512, 0)],
    ]

    def nm(prefix, si, cnt):
        return [f"{prefix}.{si}.{k}" for k in range(cnt)]

    def allnames(prefix, nsub, cnt):
        out = []
        for si in range(nsub):
            out += nm(prefix, si, cnt)
        return out

    out_chans = [f"c:out{i}" for i in range(4)]

    for ti, subs in enumerate(TILES):
        nsub = len(subs)
        has_s = any(sb_[4] for sb_ in subs)
        n0_, n1_ = subs[0][1], subs[1][1]
        FAX = HIDt[:, 14 * n0_:14 * n0_ + 2 * (TT + 2)].rearrange("p (k t) -> p k t", k=2)
        fcx0 = JC * n0_ + 14 * n1_
        FCX = HIDt[:, fcx0:fcx0 + 2 * (TT + 30)].rearrange("p (k t) -> p k t", k=2)
        assert 2 * (TT + 2) <= 8 * n0_ and 2 * (TT + 30) <= 8 * n1_
        tok0 = ti * TT
        last_tile = ti == len(TILES) - 1
        Hn = allnames("H", nsub, 8)
        YSBn = allnames("YSB", nsub, 8)
        HIDn = allnames("HID", nsub, JC)
        YSB2n = allnames("YSB2", nsub, 8)
        MIXERn = (allnames("MIX", nsub, 8) + allnames("U", nsub, 4) + allnames("AB", nsub, 2)
                  + [f"FAX.{c}.{x}" for c in range(2) for x in ["h"] + list(range(nsub))]
                  + [f"FCX.{c}.{x}" for c in range(2) for x in ["h"] + list(range(nsub))])

        xv = xT.rearrange("(k p) t -> p k t", p=128)
        def colmap():
            pieces = []
            for si, (c0, n, t0_, npr, nsa) in enumerate(subs):
                if pieces and pieces[-1][0] == "p" and pieces[-1][2] == c0 and pieces[-1][4] == t0_:
                    k_, a_, b_, ta_, tb_, sis_ = pieces[-1]
                    pieces[-1] = ("p", a_, c0 + npr, ta_, t0_ + npr, sis_ + [si])
                else:
                    pieces.append(("p", c0, c0 + npr, t0_, t0_ + npr, [si]))
                if nsa:
                    pieces.append(("s", c0 + npr, c0 + n, 0, nsa, [si]))
            return pieces

        for (k_, a_, b_, ta_, tb_, sis_) in colmap():
            nms = [x for si_ in sis_ for x in nm("XT", si_, 8)]
            if k_ == "p":
                DMA("sp", XT[:, :, a_:b_], xv[:, :, tok0 + ta_:tok0 + tb_], r=(), w=nms, chan="c:xin")
            else:
                DMA("sp", XT[:, :, a_:b_], xsT.rearrange("(k p) t -> p k t", p=128), r=(), w=nms, chan="c:xin")

        streams = [list(range(nsub)), []] if ONE_STREAM else [[0], list(range(1, nsub))]

        def Hs(si, k, a=0, b=None):
            c0, n = subs[si][0], subs[si][1]
            hv = R1t[:, 8 * c0:8 * c0 + 4 * n].bitcast(BF16)
            return hv[:, k * n + a:k * n + (n if b is None else b)]

        def Ys(si, m):
            c0, n = subs[si][0], subs[si][1]
            return R1t[:, 8 * c0 + m * n:8 * c0 + (m + 1) * n]

        def Ys2(si, m):
            c0, n = subs[si][0], subs[si][1]
            yv = HIDt[:, JC * c0:JC * c0 + 16 * n].bitcast(F32)
            return yv[:, m * n:(m + 1) * n]

        def HDs(si, j):
            c0, n = subs[si][0], subs[si][1]
            return HIDt[:, JC * c0 + j * n:JC * c0 + (j + 1) * n]
        pend = {"gen": None, "partB": None}

        def step(gen, k):
            if gen is None:
                return
            for _ in range(k):
                try:
                    next(gen)
                except StopIteration:
                    return

        tick_state = {"gen": None, "per": 0}

        def tick():
            if tick_state["gen"] is not None:
                step(tick_state["gen"], tick_state["per"])

        def gen_steps(sis):
            return 1 + KC * len(sis)

        def resolve_pending():
            if pend["partB"] is not None:
                step(pend["gen"], 10 ** 6)
                pend["partB"]()
                pend["gen"] = pend["partB"] = None

        def rstd_inplace(st, stn, n):
            t, tn = scr()
            ACT(t[:, 0:n], st[:, 0:n], AF.Ln, r=[stn], w=[tn], bias=EPS, scale=1.0)
            ACT(st[:, 0:n], t[:, 0:n], AF.Exp, r=[tn], w=[stn], scale=-0.5)

        def h_apply_sub(l2, gcol2, si, st2, st2n):
            c0, n = subs[si][0], subs[si][1]
            for k in range(KC):
                STT(Hs(si, k), XT[:, k, c0:c0 + n], vec(l2, gcol2 + k), st2[:, 0:n], ALU.mult, ALU.mult,
                    r=[f"XT.{si}.{k}", st2n, "VEC"], w=[f"H.{si}.{k}"])

        def prenorm_first(l2, gcol2):
            for si, (c0, n, t0_, npr, nsa) in enumerate(subs):
                st2, st2n = bank("st")
                for k in range(KC):
                    sq, sqn = sqs()
                    ACT(sq[:, 0:n], XT[:, k, c0:c0 + n], AF.Square, r=[f"XT.{si}.{k}"], w=[sqn])
                    MM(st2[:, 0:n], ONESB[:], sq[:, 0:n], k == 0, k == KC - 1, r=[sqn, "ONESB"], w=[st2n])
                rstd_inplace(st2, st2n, n)
                h_apply_sub(l2, gcol2, si, st2, st2n)

        class YStats:
            def __init__(self, sis):
                self.stream = 0 if (len(sis) and sis[0] == 0) else 1
                self.sts = {}
                self.pending = None
                self.cnt = {si: 0 for si in sis}

            def flush(self):
                if self.pending is not None:
                    si, sq, sqn, n = self.pending
                    st, stn = self.sts[si]
                    MM(st[:, 0:n], ONESB[:], sq[:, 0:n], self.cnt[si] == 0, self.cnt[si] == KC - 1, r=[sqn, "ONESB"], w=[stn])
                    self.cnt[si] += 1
                    self.pending = None

            def add(self, si, src_ap, src_res, n):
                if si not in self.sts:
                    self.sts[si] = bank("st", hold=True)
                sq, sqn = sqs(self.stream)
                ACT(sq[:, 0:n], src_ap, AF.Square, r=src_res, w=[sqn])
                self.flush()
                self.pending = (si, sq, sqn, n)

        class Bnd:
            def __init__(self, sis, gains, Y, yname, ys, nxt):
                self.sis, self.gains, self.Y, self.yname, self.ys, self.nxt = sis, gains, Y, yname, ys, nxt

            def partA(self):
                for si in self.sis:
                    st, stn = self.ys.sts[si]
                    rstd_inplace(st, stn, subs[si][1])
                yield
                LAG = 3
                for si in self.sis:
                    c0, n = subs[si][0], subs[si][1]
                    st, stn = self.ys.sts[si]

                    def square(m, si=si, c0=c0, n=n):
                        ww = [f"H.{si}.{m}"] + ([f"YSB.{si}.{m // 2}"] if self.yname == "YSB" else [])
                        ACT(Hs(si, m), XT[:, m, c0:c0 + n], AF.Square, r=[f"XT.{si}.{m}"], w=ww)

                    for m in range(KC + LAG):
                        if m < KC:
                            t, tn = scr()
                            STT(t[:, 0:n], self.Y(si, m), self.gains[:, m:m + 1], st[:, 0:n], ALU.mult, ALU.mult,
                                r=[f"{self.yname}.{si}.{m}", stn, "VEC", "VECH"], w=[tn])
                            TT_(XT[:, m, c0:c0 + n], XT[:, m, c0:c0 + n], t[:, 0:n], ALU.add,
                                r=[tn, f"XT.{si}.{m}"], w=[f"XT.{si}.{m}"], eng="pool")
                        if self.nxt is not None and m >= LAG:
                            square(m - LAG)
                        if m < KC:
                            yield
                    bank_release(stn)

            def partB(self):
                if self.nxt is None:
                    return
                for si in self.sis:
                    c0, n = subs[si][0], subs[si][1]
                    st2, st2n = bank("st")
                    for m in range(KC):
                        MM(st2[:, 0:n], ONESB[:], Hs(si, m), m == 0, m == KC - 1, r=[f"H.{si}.{m}", "ONESB"], w=[st2n])
                    rstd_inplace(st2, st2n, n)
                    h_apply_sub(self.nxt[0], self.nxt[1], si, st2, st2n)

        def run_phase(l, keys, work, head=0, tail=0, ys=None, mk_bnd=None, gpc=1):
            nk = len(keys)
            if head + tail >= nk:
                blocks = [("HT", list(range(nk)))]
            else:
                blocks = []
                if head:
                    blocks.append(("H", list(range(head))))
                blocks.append(("M", list(range(head, nk - tail))))
                if tail:
                    blocks.append(("T", list(range(nk - tail, nk))))
            for kind, idxs in blocks:
                if kind == "M":
                    for p in idxs:
                        slot, sn, wi = wload(l, keys[p])
                        for stm in streams:
                            for si in stm:
                                work(p, si, slot, sn)
                        wrelease(wi)
                    continue
                is_head = "H" in kind
                is_tail = ("T" in kind) and (mk_bnd is not None)
                loaded = [(p,) + wload(l, keys[p]) for p in idxs]
                if is_head and pend["gen"] is not None:
                    ngrp = max(1, len(loaded) * len(streams[0]) * gpc)
                    tick_state["gen"] = pend["gen"]
                    tick_state["per"] = -(-gen_steps(streams[1]) // ngrp)
                for (p, slot, sn, wi) in loaded:
                    for si in streams[0]:
                        work(p, si, slot, sn)
                tick_state["gen"] = None
                if is_head:
                    resolve_pending()
                genA = None
                bA = mk_bnd(streams[0], ys[0]) if is_tail else None
                if is_tail:
                    ys[0].flush()
                    genA = bA.partA()
                    step(genA, 1)
                    ngrp = max(1, len(loaded) * len(streams[1]) * gpc)
                    tick_state["gen"] = genA
                    tick_state["per"] = -(-gen_steps(streams[0]) // ngrp)
                calls = [(p, si, slot, sn) for (p, slot, sn, wi) in loaded for si in streams[1]]
                partb_done = False
                for ci, (p, si, slot, sn) in enumerate(calls):
                    if is_tail and EARLY_PARTB and len(calls) >= 3 and ci == len(calls) - 1:
                        tick_state["gen"] = None
                        step(genA, 10 ** 6)
                        bA.partB()
                        partb_done = True
                    work(p, si, slot, sn)
                tick_state["gen"] = None
                if is_tail:
                    if not partb_done:
                        step(genA, 10 ** 6)
                        bA.partB()
                    ys[1].flush()
                    bB = mk_bnd(streams[1], ys[1])
                    pend["gen"] = bB.partA()
                    step(pend["gen"], 1)
                    pend["partB"] = bB.partB
                for (p, slot, sn, wi) in loaded:
                    wrelease(wi)

        def ffn(l, f, gains_post, nxt):
            def up_work(jj, si, slot, sn):
                c0, n = subs[si][0], subs[si][1]
                wv = slot[:, 0:4096].rearrange("p (a k c) -> p a k c", a=2, k=KC)
                for jl in range(2):
                    j = 2 * jj + jl
                    gp, gpn = bank("mm")
                    up, upn = bank("mm")
                    for k in range(KC):
                        MM(gp[:, 0:n], wv[:, 0, k, jl * 128:(jl + 1) * 128], Hs(si, k), k == 0, k == KC - 1,
                           r=[sn, f"H.{si}.{k}"], w=[gpn])
                    for k in range(KC):
                        MM(up[:, 0:n], wv[:, 1, k, jl * 128:(jl + 1) * 128], Hs(si, k), k == 0, k == KC - 1,
                           r=[sn, f"H.{si}.{k}"], w=[upn])
                    t, tn = scr()
                    ACT(t[:, 0:n], gp[:, 0:n], AF.Silu, r=[gpn], w=[tn])
                    TT_(HDs(si, j), t[:, 0:n], up[:, 0:n], ALU.mult, r=[tn, upn], w=[f"HID.{si}.{j}"])
                    tick()

            run_phase(l, [(f"f{f}u", jj) for jj in range(11)], up_work, head=3, tail=0, gpc=2)
            mark(f"t{ti}l{l}.f{f}down")
            if f == 1:
                for ch in range(2):
                    for k in range(3):
                        TS(DIAGA[:, ch, k, :], IDENT, vec(l, V_ACW + ch * 3 + k), None, ALU.mult, None,
                           r=["CST", "VEC"], w=[f"DIAGA.{ch}"])
                    for k in range(31):
                        TS(DIAGC[:, ch, k, :], IDENT, vec(l, V_CCW + ch * 31 + k), None, ALU.mult, None,
                           r=["CST", "VEC"], w=[f"DIAGC.{ch}"])
            S.transfer(Hn, YSBn)
            ys = (YStats(streams[0]), YStats(streams[1]))

            def down_work(m, si, slot, sn):
                c0, n = subs[si][0], subs[si][1]
                wv = slot[:, 0:2816].rearrange("p (j c) -> p j c", j=JC)
                yp, ypn = bank("mm")
                for j in range(JC):
                    MM(yp[:, 0:n], wv[:, j, :], HDs(si, j), j == 0, j == JC - 1,
                       r=[sn, f"HID.{si}.{j}"], w=[ypn])
                ACT(Ys(si, m), yp[:, 0:n], AF.Copy, r=[ypn], w=[f"YSB.{si}.{m}"])
                ys[0 if si in streams[0] else 1].add(si, yp[:, 0:n], [ypn], n)
                tick()

            run_phase(l, [(f"f{f}d", m) for m in range(KC)], down_work, head=0, tail=3, ys=ys,
                      mk_bnd=lambda sis, y: Bnd(sis, gains_post, Ys, "YSB", y, nxt))

        prenorm_first(0, G_F1PRE)

        for l in range(L):
            if has_s:
                DMA("sp", CBt[:, 0:960].rearrange("p (c x) -> p c x", c=2),
                    scT[l].rearrange("(c p) k s -> p c (k s)", p=128), r=(), w=["CB.0", "CB.1"], chan="c:lay")
                DMA("sp", CBt[:, 960:1024].rearrange("p (c x) -> p c x", c=2),
                    saT[l].rearrange("(c p) k s -> p c (k s)", p=128), r=(), w=["CB.1"], chan="c:lay")
                S.add("act", lambda e: e.activation(out=FCS[:, :, 0:30, :], in_=CBt[:, 0:960].rearrange("p (c k s) -> p c k s", c=2, k=30), func=AF.Copy),
                      r=["CB.0", "CB.1"], w=["FCS.h"])
                S.add("act", lambda e: e.activation(out=FAS[:, :, 0:2, :], in_=CBt[:, 960:1024].rearrange("p (c k s) -> p c k s", c=2, k=2), func=AF.Copy),
                      r=["CB.1"], w=["FAS.h"])
                DMA("sp", cash[l], sa[l][:, 1:2, :], r=(), w=(), chan="c:out")
                DMA("sp", ccsh[l], sc[l][:, 1:30, :], r=(), w=(), chan="c:out")
                DMA("sp", LNG[0:NS, :], ROWS[l, 0:1, :].partition_broadcast(NS), r=(), w=["CB.0"], chan="c:lay")
                DMA("sp", LNB[0:NS, :], ROWS[l, 1:2, :].partition_broadcast(NS), r=(), w=["CB.1"], chan="c:lay")

            mark(f"t{ti}l{l}.ffn1")
            ffn(l, 1, VECH[:, l * 16:l * 16 + 8], (l, G_MPRE))

            mark(f"t{ti}l{l}.mixA")
            S.transfer(HIDn, MIXERn)
            pv = pT[l].rearrange("(k p) t -> p k t", p=128)
            for (k_, a_, b_, ta_, tb_, sis_) in colmap():
                if k_ == "p":
                    DMA("pool", PT[:, :, a_:b_], pv[:, :, tok0 + ta_:tok0 + tb_], r=(), w=["PT"], chan="c:pin")
                else:
                    DMA("pool", PT[:, :, a_:b_], psT[l].rearrange("(k p) t -> p k t", p=128), r=(), w=["PT"], chan="c:pin")
            if ti == 0:
                MEMSET(FAX[:, :, 0:2], 0.0, ["FAX.0.h", "FAX.1.h"])
                MEMSET(FCX[:, :, 0:30], 0.0, ["FCX.0.h", "FCX.1.h"])
            else:
                ACT(FAX[:, :, 0:2], FAH[:, l, :, :], AF.Copy, r=["FAH"], w=["FAX.0.h", "FAX.1.h"])
                ACT(FCX[:, :, 0:30], FCH[:, l, :, :], AF.Copy, r=["FCH"], w=["FCX.0.h", "FCX.1.h"])

            bbc, bbcn = scr()
            DMA("sp", bbc[:], ROWS[l, 2:3, :].partition_broadcast(128), r=(), w=[bbcn], chan="c:lay")
            prs, prsn = bank("mm")
            MM(prs[:, :], ONESB[:], WSTBt[:, l * 512:(l + 1) * 512], True, True, r=["ONESB", "WSTB"], w=[prsn])
            rsum, rsumn = scr()
            ACT(rsum[:], prs[:], AF.Copy, r=[prsn], w=[rsumn], scale=float(D))
            for hd in range(4):
                hs = slice(hd * 128, (hd + 1) * 128)
                STT(CH[:, hd, :], rsum[:, hs], vec(l, V_BLB + hd), bbc[:, hs], ALU.mult, ALU.add,
                    r=[rsumn, bbcn, "VEC"], w=["CH"])
            if has_s:
                for hd in range(4):
                    STT(S1[:, hd:hd + 1], W00T[:, l * 4 + hd:l * 4 + hd + 1], vec(l, V_BLB + hd), bbc[:, hd * 128:hd * 128 + 1],
                        ALU.mult, ALU.add, r=["W00T", "VEC", bbcn], w=["S1"])
                    TS(CHS[:, hd, :], ONES16[:], S1[:, hd:hd + 1], None, ALU.mult, None, r=["ONES16", "S1"], w=["CHS"])

            def prev(prefix, ch, si):
                return f"{prefix}.{ch}.{'h' if si == 0 else si - 1}"

            def conv_a(ch, si):
                c0, n, t0_, npr, nsa = subs[si]
                cv, cvn = bank("mm")
                for k in range(3):
                    MM(cv[:, 0:npr], DIAGA[:, ch, k, :], FAX[:, ch, t0_ + k:t0_ + k + npr], k == 0, k == 2,
                       r=[f"FAX.{ch}.{si}", prev("FAX", ch, si), f"DIAGA.{ch}"], w=[cvn])
                if nsa:
                    for k in range(3):
                        MM(cv[:, npr:n], DIAGA[:, ch, k, :], FAS[:, ch, k, :], k == 0, k == 2,
                           r=["FAS.h", f"FAS.n.{ch}", f"DIAGA.{ch}"], w=[cvn])
                TT_(HDs(si, ch), HDs(si, 12 + ch), cv[:, 0:n], ALU.mult,
                    r=[f"AB.{si}.{ch}", cvn], w=[f"MIX.{si}.{ch}"])

            deferred = [None]

            def defer(fn):
                if deferred[0] is not None:
                    deferred[0]()
                deferred[0] = fn

            def a_work(ch, si, slot, sn):
                c0, n, t0_, npr, nsa = subs[si]
                wv = slot[:, 0:3072].rearrange("p (k c) -> p k c", k=KC)
                pv_, pvn = bank("mm")
                pc, pcn = bank("mm")
                pb, pbn = bank("mm")
                for (pp, ppn, co) in ((pv_, pvn, 0), (pc, pcn, 128), (pb, pbn, 256)):
                    for k in range(KC):
                        MM(pp[:, 0:n], wv[:, k, co:co + 128], Hs(si, k), k == 0, k == KC - 1,
                           r=[sn, f"H.{si}.{k}"], w=[ppn])
                av, avn = scr()
                ACT(av[:, 0:n], pv_[:, 0:n], AF.Copy, r=[pvn], w=[avn])
                TT_(FAX[:, ch, 2 + t0_:2 + t0_ + npr], av[:, 0:npr], pc[:, 0:npr], ALU.mult, r=[avn, pcn], w=[f"FAX.{ch}.{si}"])
                if last_tile and si == nsub - 1:
                    TT_(CAO[:, l, ch, :], av[:, npr - 2:npr], pc[:, npr - 2:npr], ALU.mult, r=[avn, pcn], w=[f"CAO.{l}"])
                if nsa:
                    TT_(FAS[:, ch, 2, :], av[:, npr:n], pc[:, npr:n], ALU.mult, r=[avn, pcn], w=[f"FAS.n.{ch}"])
                    TT_(CASN[:, l, ch, :], av[:, npr:n], pc[:, npr:n], ALU.mult, r=[avn, pcn], w=[f"CASN.{l}"])
                ACT(HDs(si, 12 + ch), pb[:, 0:n], AF.Copy, r=[pbn], w=[f"AB.{si}.{ch}"])
                defer(lambda ch=ch, si=si: conv_a(ch, si))
                tick()

            run_phase(l, [("inA", 0), ("inA", 1)], a_work, head=2, tail=0)
            defer(None)

            mark(f"t{ti}l{l}.mixUV")
            slotU, snU, wiU = wload(l, ("inU", 0))
            wvu = slotU[:, 0:4096].rearrange("p (k c) -> p k c", k=KC)
            slotV, snV, wiV = wload(l, ("inV", 0))
            wvv = slotV[:, 0:4096].rearrange("p (k c) -> p k c", k=KC)

            goff = [0]
            for sb_ in subs:
                goff.append(goff[-1] + sb_[3] // 128 + (1 if sb_[4] else 0))
            assert goff[-1] <= 9

            def gidx(si, g):
                return goff[si] + g

            def groups(si):
                c0, n, t0_, npr, nsa = subs[si]
                gs = [(g, g * 128, 128, False) for g in range(npr // 128)]
                if nsa:
                    gs.append((npr // 128, npr, nsa, True))
                return gs

            def u_chunk(si, c):
                c0, n, t0_, npr, nsa = subs[si]
                pu, pun = bank("mm")
                for k in range(KC):
                    MM(pu[:, 0:n], wvu[:, k, c * 128:(c + 1) * 128], Hs(si, k), k == 0, k == KC - 1,
                       r=[snU, f"H.{si}.{k}"], w=[pun])
                ACT(HDs(si, 8 + c), pu[:, 0:n], AF.Gelu_apprx_tanh, r=[pun], w=[f"U.{si}.{c}"])

            vdef = [None]

            def vdefer(fn):
                if vdef[0] is not None:
                    vdef[0]()
                vdef[0] = fn

            def v_group(si, g):
                _g, gc0, nt, is_s = groups(si)[g]
                gi = gidx(si, g)
                pb_ = gi % 2
                STATS, MV = STATS2[:, pb_], MV2[:, pb_]
                VE, RS = VEt[:, pb_ * 4:pb_ * 4 + 4], RSt[:, pb_ * 4:pb_ * 4 + 4]
                pvv, pvvn = bank("mm")
                for k in range(KC):
                    MM(pvv[0:nt, :], Hs(si, k, gc0, gc0 + nt), wvv[:, k, :], k == 0, k == KC - 1,
                       r=[snV, f"H.{si}.{k}"], w=[pvvn])
                gv, gvn = scr()
                ACT(gv[0:nt, :], pvv[0:nt, :], AF.Gelu_apprx_tanh, r=[pvvn], w=[gvn])
                for hd in range(4):
                    S.add("dve", lambda e, hd=hd, gv=gv, nt=nt, STATS=STATS: e.bn_stats(out=STATS[0:nt, hd, :], in_=gv[0:nt, hd * 128:(hd + 1) * 128]),
                          r=[gvn], w=[f"STATS.{pb_}.{hd}"])
                for hd in range(4):
                    S.add("dve", lambda e, hd=hd, nt=nt, STATS=STATS, MV=MV: e.bn_aggr(out=MV[0:nt, hd, :], in_=STATS[0:nt, hd, :]),
                          r=[f"STATS.{pb_}.{hd}"], w=[f"MV.{pb_}.{hd}"])
                mvn = [f"MV.{pb_}.{hd}" for hd in range(4)]
                TS(VE[0:nt, :], MV[0:nt, :, 1], EPS, None, ALU.add, None, r=mvn, w=[f"VE.{pb_}"])
                IT, T4 = ITt[:, pb_ * 4:pb_ * 4 + 4], T4t[:, pb_ * 4:pb_ * 4 + 4]
                itn, t4n, rsn, ven = f"IT.{pb_}", f"T4.{pb_}", f"RS.{pb_}", f"VE.{pb_}"
                TS(IT[0:nt, :], VE[0:nt, :].bitcast(mybir.dt.int32), 1, None, ALU.logical_shift_right, None, r=[ven], w=[itn])
                TS(IT[0:nt, :], IT[0:nt, :], -1.0, 1597463007.0, ALU.mult, ALU.add, r=[itn], w=[itn])
                cur, curn = IT[0:nt, :].bitcast(F32), itn
                for _it in range(2):
                    TT_(T4[0:nt, :], cur, cur, ALU.mult, r=[curn], w=[t4n], eng="pool")
                    TT_(T4[0:nt, :], T4[0:nt, :], VE[0:nt, :], ALU.mult, r=[t4n, ven], w=[t4n], eng="pool")
                    TS(T4[0:nt, :], T4[0:nt, :], -0.5, 1.5, ALU.mult, ALU.add, r=[t4n], w=[t4n], eng="pool")
                    TT_(RS[0:nt, :], cur, T4[0:nt, :], ALU.mult, r=[curn, t4n], w=[rsn], eng="pool")
                    cur, curn = RS[0:nt, :], rsn

                def normalize(si=si, g=g, gi=gi, nt=nt, gv=gv, gvn=gvn, is_s=is_s, MV=MV, RS=RS, pb_=pb_):
                    for hd in range(4):
                        hs = slice(hd * 128, (hd + 1) * 128)
                        TS(V[0:nt, gi, hs], gv[0:nt, hs], MV[0:nt, hd, 0:1], RS[0:nt, hd:hd + 1], ALU.subtract, ALU.mult,
                           r=[gvn, f"MV.{pb_}.{hd}", f"RS.{pb_}"], w=[f"V.{gi}"])
                    if is_s:
                        vn, vnn = scr()
                        for hd in range(4):
                            hs = slice(hd * 128, (hd + 1) * 128)
                            TS(vn[0:nt, hs], gv[0:nt, hs], MV[0:nt, hd, 0:1], RS[0:nt, hd:hd + 1], ALU.subtract, ALU.mult,
                               r=[gvn, f"MV.{pb_}.{hd}", f"RS.{pb_}"], w=[vnn])
                        TT_(vn[0:nt, :], vn[0:nt, :], LNG[0:nt, :], ALU.mult, r=[vnn, "CB.0"], w=[vnn])
                        TT_(vn[0:nt, :], vn[0:nt, :], LNB[0:nt, :], ALU.add, r=[vnn, "CB.1"], w=[vnn])
                        DMA("sp", vs[l], vn[0:nt, :], r=[vnn], w=(), chan="c:out")
                vdefer(normalize)

            def spatial(si):
                c0, n, t0_, npr, nsa = subs[si]
                ngp = npr // 128
                for hd in range(4):
                    sp, spn = bank("mm")
                    for (g, gc0, nt, is_s) in groups(si):
                        gi = gidx(si, g)
                        rhs, rr = (WSS[:, l, hd, :], ["WSS"]) if is_s else (WSTB[:, l, hd, :], ["WSTB"])
                        MM(sp[:, gc0:gc0 + nt], V[0:nt, gi, hd * 128:(hd + 1) * 128], rhs, True, True,
                           r=[f"V.{gi}"] + rr, w=[spn])
                    gcol = vec(l, V_BLG + hd)
                    spv = sp[:, 0:npr].rearrange("p (g i) -> p g i", g=ngp)
                    chb = CHt[:, hd * 128:(hd + 1) * 128].rearrange("p (o i) -> p o i", o=1).to_broadcast([128, ngp, 128])
                    STT(spv, spv, gcol, chb, ALU.mult, ALU.add, r=[spn, "VEC", "CH"], w=[spn])
                    if nsa:
                        STT(sp[:, npr:n], sp[:, npr:n], gcol, CHS[:, hd, :], ALU.mult, ALU.add, r=[spn, "VEC", "CHS"], w=[spn])
                    TT_(HDs(si, 2 + hd), HDs(si, 8 + hd), sp[:, 0:n], ALU.mult,
                        r=[f"U.{si}.{hd}", spn], w=[f"MIX.{si}.{2 + hd}"])

            for si in range(nsub):
                ngs = len(groups(si))
                for q in range(max(4, ngs)):
                    if q < ngs:
                        v_group(si, q)
                    if q < 4:
                        u_chunk(si, q)
            vdefer(None)
            wrelease(wiU)
            wrelease(wiV)

            mark(f"t{ti}l{l}.mixC")
            slot, sn, wiC = wload(l, ("inC", 0))
            wv = slot[:, 0:4096].rearrange("p (k c) -> p k c", k=KC)

            def c_proj(si):
                c0, n, t0_, npr, nsa = subs[si]
                for ch in range(2):
                    pval, pvaln = bank("mm")
                    pg, pgn = bank("mm")
                    for (pp, ppn, co) in ((pval, pvaln, ch * 128), (pg, pgn, 256 + ch * 128)):
                        for k in range(KC):
                            MM(pp[:, 0:n], wv[:, k, co:co + 128], Hs(si, k), k == 0, k == KC - 1,
                               r=[sn, f"H.{si}.{k}"], w=[ppn])
                    sg, sgn = scr()
                    ACT(sg[:, 0:n], pg[:, 0:n], AF.Sigmoid, r=[pgn], w=[sgn])
                    TT_(FCX[:, ch, 30 + t0_:30 + t0_ + npr], sg[:, 0:npr], pval[:, 0:npr], ALU.mult, r=[sgn, pvaln], w=[f"FCX.{ch}.{si}"])
                    if last_tile and si == nsub - 1:
                        TT_(CCO[:, l, ch, :], sg[:, npr - 30:npr], pval[:, npr - 30:npr], ALU.mult, r=[sgn, pvaln], w=[f"CCO.{l}"])
                    if nsa:
                        TT_(FCS[:, ch, 30, :], sg[:, npr:n], pval[:, npr:n], ALU.mult, r=[sgn, pvaln], w=[f"FCS.n.{ch}"])
                        TT_(CCSN[:, l, ch, :], sg[:, npr:n], pval[:, npr:n], ALU.mult, r=[sgn, pvaln], w=[f"CCSN.{l}"])

            def conv_c(si):
                c0, n, t0_, npr, nsa = subs[si]
                for ch in range(2):
                    cv, cvn = bank("mm")
                    for k in range(31):
                        MM(cv[:, 0:npr], DIAGC[:, ch, k, :], FCX[:, ch, t0_ + k:t0_ + k + npr], k == 0, k == 30,
                           r=[f"FCX.{ch}.{si}", prev("FCX", ch, si), f"DIAGC.{ch}"], w=[cvn])
                    if nsa:
                        for k in range(31):
                            MM(cv[:, npr:n], DIAGC[:, ch, k, :], FCS[:, ch, k, :], k == 0, k == 30,
                               r=["FCS.h", f"FCS.n.{ch}", f"DIAGC.{ch}"], w=[cvn])
                    ACT(CB[:, ch, 0:n], cv[:, 0:n], AF.Identity, r=[cvn, "VEC"], w=[f"CB.{ch}"], bias=vec(l, V_CCB + ch), scale=1.0)
                    ACT(CBR[:, ch, 0:n], CB[:, ch, 0:n], AF.Copy, r=[f"CB.{ch}"], w=[f"CBR.{ch}"])
                    ACT(CSQ[:, ch, 0:n], CB[:, ch, 0:n], AF.Square, r=[f"CB.{ch}"], w=[f"CSQ.{ch}"])

            def ln_c(si):
                c0, n, t0_, npr, nsa = subs[si]
                mp, mpn = bank("st")
                ep, epn = bank("st")
                for ch in range(2):
                    MM(mp[:, 0:n], ONES256[:], CBR[:, ch, 0:n], ch == 0, ch == 1, r=[f"CBR.{ch}", "ONES256"], w=[mpn])
                for ch in range(2):
                    MM(ep[:, 0:n], ONES256[:], CSQ[:, ch, 0:n], ch == 0, ch == 1, r=[f"CSQ.{ch}", "ONES256"], w=[epn])
                ms, msn = scr()
                ACT(ms[:, 0:n], mp[:, 0:n], AF.Square, r=[mpn], w=[msn])
                TT_(ms[:, 0:n], ep[:, 0:n], ms[:, 0:n], ALU.subtract, r=[epn, msn], w=[msn])
                ACT(ms[:, 0:n], ms[:, 0:n], AF.Ln, r=[msn], w=[msn], bias=EPS, scale=1.0)
                ACT(ep[:, 0:n], ms[:, 0:n], AF.Exp, r=[msn], w=[epn], scale=-0.5)
                for ch in range(2):
                    t1, t1n = scr()
                    TT_(t1[:, 0:n], CB[:, ch, 0:n], mp[:, 0:n], ALU.subtract, r=[f"CB.{ch}", mpn], w=[t1n])
                    TT_(t1[:, 0:n], t1[:, 0:n], ep[:, 0:n], ALU.mult, r=[t1n, epn], w=[t1n])
                    ACT(HDs(si, 6 + ch), t1[:, 0:n], AF.Silu, r=[t1n, "VEC"], w=[f"MIX.{si}.{6 + ch}"],
                        bias=vec(l, V_CLB + ch), scale=vec(l, V_CLG + ch))

            for si in range(nsub):
                c_proj(si)
                spatial(si)
                if si > 1:
                    ln_c(si - 2)
                if si > 0:
                    conv_c(si - 1)
            wrelease(wiC)
            if nsub > 1:
                ln_c(nsub - 2)
            conv_c(nsub - 1)
            ln_c(nsub - 1)

            if not last_tile:
                ACT(FAH[:, l, :, :], FAX[:, :, TT:TT + 2], AF.Copy, r=[f"FAX.0.{nsub - 1}", f"FAX.1.{nsub - 1}"], w=["FAH"])
                ACT(FCH[:, l, :, :], FCX[:, :, TT:TT + 30], AF.Copy, r=[f"FCX.0.{nsub - 1}", f"FCX.1.{nsub - 1}"], w=["FCH"])
            else:
                DMA("sp", cap[l].rearrange("(c p) k -> p c k", p=128), CAO[:, l, :, :], r=[f"CAO.{l}"], w=(), chan="c:out")
                DMA("sp", ccp[l].rearrange("(c p) k -> p c k", p=128), CCO[:, l, :, :], r=[f"CCO.{l}"], w=(), chan="c:out")
            if has_s:
                DMA("sp", casn[l].rearrange("(c p) s -> p c s", p=128), CASN[:, l, :, :], r=[f"CASN.{l}"], w=(), chan="c:out")
                DMA("sp", ccsn[l].rearrange("(c p) s -> p c s", p=128), CCSN[:, l, :, :], r=[f"CCSN.{l}"], w=(), chan="c:out")

            mark(f"t{ti}l{l}.wout")
            S.transfer(Hn, YSBn)
            ys = (YStats(streams[0]), YStats(streams[1]))

            def out_work(hh, si, slot, sn, ys=ys):
                c0, n, t0_, npr, nsa = subs[si]
                wv = slot[:, 0:4096].rearrange("p (k c) -> p k c", k=KC)
                for ml in range(4):
                    m = 4 * hh + ml
                    yp, ypn = bank("mm")
                    for k in range(KC):
                        MM(yp[:, 0:n], wv[:, k, ml * 128:(ml + 1) * 128], HDs(si, k), k == 0, k == KC - 1,
                           r=[sn, f"MIX.{si}.{k}"], w=[ypn])
                    ACT(Ys(si, m), yp[:, 0:n], AF.Copy, r=[ypn], w=[f"YSB.{si}.{m}"])
                    ys[0 if si in streams[0] else 1].add(si, yp[:, 0:n], [ypn], n)
                    tick()

            gm = vec(l, G_MPOST, 8)
            run_phase(l, [("out", 0), ("out", 1)], out_work, head=0, tail=2, ys=ys, gpc=4,
                      mk_bnd=lambda sis, y, gm=gm, l=l: Bnd(sis, gm, Ys, "YSB", y, (l, G_F2PRE)))
            S.transfer(MIXERn, HIDn)

            mark(f"t{ti}l{l}.ffn2")
            ffn(l, 2, VECH[:, l * 16 + 8:l * 16 + 16], (l, G_EPRE))

            mark(f"t{ti}l{l}.ple")
            S.transfer(HIDn, YSB2n)
            pslot, psn, wiP = wload(l, ("ewp", 0))
            wp = pslot[:, 0:2048].rearrange("p (k c) -> p k c", k=2)
            ys = (YStats(streams[0]), YStats(streams[1]))

            def ple_work(hh, si, slot, sn, ys=ys, wp=wp, psn=psn):
                c0, n, t0_, npr, nsa = subs[si]
                wv = slot[:, 0:4096].rearrange("p (k c) -> p k c", k=KC)
                ptn = "PT"
                for ml in range(4):
                    m = 4 * hh + ml
                    gp, gpn = bank("mm")
                    pp, ppn = bank("mm")
                    for k in range(KC):
                        MM(gp[:, 0:n], wv[:, k, ml * 128:(ml + 1) * 128], Hs(si, k), k == 0, k == KC - 1,
                           r=[sn, f"H.{si}.{k}"], w=[gpn])
                    for k in range(2):
                        MM(pp[:, 0:n], wp[:, k, m * 128:(m + 1) * 128], PT[:, k, c0:c0 + n], k == 0, k == 1,
                           r=[psn, ptn], w=[ppn])
                    t, tn = scr()
                    ACT(t[:, 0:n], gp[:, 0:n], AF.Sigmoid, r=[gpn], w=[tn])
                    TT_(Ys2(si, m), t[:, 0:n], pp[:, 0:n], ALU.mult, r=[tn, ppn], w=[f"YSB2.{si}.{m}"])
                    ys[0 if si in streams[0] else 1].add(si, Ys2(si, m), [f"YSB2.{si}.{m}"], n)
                    tick()

            nxt = (l + 1, G_F1PRE) if l + 1 < L else None
            ge = vec(l, G_EPOST, 8)
            run_phase(l, [("ewg", 0), ("ewg", 1)], ple_work, head=2, tail=2, ys=ys, gpc=4,
                      mk_bnd=lambda sis, y, ge=ge, nxt=nxt: Bnd(sis, ge, Ys2, "YSB2", y, nxt))
            wrelease(wiP)
            S.transfer(YSB2n, HIDn)

        mark(f"t{ti}.end")
        resolve_pending()
        yv = yT.rearrange("(k p) t -> p k t", p=128)
        for (k_, a_, b_, ta_, tb_, sis_) in colmap():
            nms = [x for si_ in sis_ for x in nm("XT", si_, 8)]
            if k_ == "p":
                DMA("sp", yv[:, :, tok0 + ta_:tok0 + tb_], XT[:, :, a_:b_], r=nms, w=(), chan="c:out")
            else:
                DMA("sp", ysT.rearrange("(k p) t -> p k t", p=128), XT[:, :, a_:b_], r=nms, w=(), chan="c:out")

    S.finalize()
    keys = set(S.chan_cnt.keys()) | {"pe", "act", "dve", "pool"}
    sems = {k: es.enter_context(nc.semaphore(k.replace(":", "_"))) for k in sorted(keys)}
    with nc.Block() as block:
        @block.sync
        def _(eng):
            S.emit("sp", eng, sems, final_waits=out_chans)

        @block.gpsimd
        def _(eng):
            S.emit("pool", eng, sems)

        @block.scalar
        def _(eng):
            S.emit("act", eng, sems)

        @block.vector
        def _(eng):
            S.emit("dve", eng, sems)

        @block.tensor
        def _(eng):
            S.emit("pe", eng, sems)
    es.close()
    return nc


def kernel(**inp):
    inp = {k: np.asarray(v) for k, v in inp.items()}
    Wp = pack_weights(inp)
    VECp = pack_vec(inp)
    ROWS = np.ascontiguousarray(np.stack([inp["b_ln_g"].reshape(L, 512), inp["b_ln_b"].reshape(L, 512),
                                          inp["b_bias"].reshape(L, 512)], axis=1)).astype(np.float32)
    WST = np.ascontiguousarray(inp["b_ws"].transpose(0, 3, 1, 2).reshape(L, 128, 512)).astype(np.float32)
    WS00 = np.ascontiguousarray(inp["b_ws"][:, :, 0, 0].reshape(1, L * 4)).astype(np.float32)
    CONSTS = np.concatenate([np.eye(128, dtype=np.float32), np.triu(np.ones((128, 128), np.float32))], axis=1)
    in_maps = []
    for c in range(NCORES):
        ss = slice(NS * c, NS * (c + 1))
        in_maps.append({
            "xT": np.ascontiguousarray(inp["x_prompt"][c].T),
            "xsT": np.ascontiguousarray(inp["x_sample"][ss, 0, :].T),
            "pT": np.ascontiguousarray(inp["p_prompt"][:, c].transpose(0, 2, 1)),
            "psT": np.ascontiguousarray(inp["p_sample"][:, ss, 0, :].transpose(0, 2, 1)),
            "sa": np.ascontiguousarray(inp["state_conv_a"][:, ss]),
            "saT": np.ascontiguousarray(inp["state_conv_a"][:, ss].transpose(0, 3, 2, 1)),
            "sc": np.ascontiguousarray(inp["state_conv_c"][:, ss]),
            "scT": np.ascontiguousarray(inp["state_conv_c"][:, ss].transpose(0, 3, 2, 1)),
            "W": Wp, "VEC": VECp, "ROWS": ROWS, "WST": WST, "WS00": WS00, "CONSTS": CONSTS,
        })
    nc = build_program()
    res = run_bass_kernel_spmd(nc, in_maps, core_ids=list(range(NCORES)))
    R = res.results
    y_prompt = np.stack([R[c]["yT"].T for c in range(NCORES)], axis=0)
    y_sample = np.concatenate([R[c]["ysT"].T for c in range(NCORES)], axis=0)[:, None, :]
    conv_a_prompt = np.stack([R[c]["cap"].transpose(0, 2, 1) for c in range(NCORES)], axis=1)
    conv_c_prompt = np.stack([R[c]["ccp"].transpose(0, 2, 1) for c in range(NCORES)], axis=1)
    conv_a_sample = np.concatenate(
        [np.concatenate([R[c]["cash"], R[c]["casn"].transpose(0, 2, 1)[:, :, None, :]], axis=2) for c in range(NCORES)], axis=1)
    conv_c_sample = np.concatenate(
        [np.concatenate([R[c]["ccsh"], R[c]["ccsn"].transpose(0, 2, 1)[:, :, None, :]], axis=2) for c in range(NCORES)], axis=1)
    v_rows = np.concatenate([R[c]["vs"] for c in range(NCORES)], axis=1)[:, :, None, :]
    f = lambda a: np.ascontiguousarray(a, dtype=np.float32)
    return (f(y_prompt), f(y_sample), f(conv_a_prompt), f(conv_c_prompt), f(conv_a_sample), f(conv_c_sample), f(v_rows))
```
